# Optimizing a Trainium2 kernel written in Bass

```python
import math
import jax, jax.numpy as jnp
from jax import lax
import numpy as np

D_MODEL = 1024
BATCH = 8
SEQ = 2048
DEPTH = 1

CHUNK = 128
A_GROUPS = 8
A_GROUP_DIM = D_MODEL // A_GROUPS
A_WIDTH = A_GROUPS * A_GROUP_DIM
B_HEADS = 16
B_HEAD_DIM = D_MODEL // B_HEADS
B_WIDTH = B_HEADS * B_HEAD_DIM
DILATED_PATTERNS = ((128, 1), (512, 4), (2048, 16))
N_BRANCHES = 2
D_FF = 4 * D_MODEL
EPS = 1e-6
IN_SPLITS = (A_WIDTH, 2 * A_WIDTH, 2 * A_WIDTH + B_WIDTH, 2 * A_WIDTH + 2 * B_WIDTH,
             2 * A_WIDTH + 3 * B_WIDTH, 2 * A_WIDTH + 3 * B_WIDTH + D_MODEL)
IN_COLS = 2 * A_WIDTH + 3 * B_WIDTH + N_BRANCHES * D_MODEL

kernel_name = "hybrid_gmlp_dilated_alibi_block"


def rms_norm(x, g):
    xf = x.astype(jnp.float32)
    y = xf * lax.rsqrt(jnp.mean(xf * xf, axis=-1, keepdims=True) + EPS)
    return y.astype(x.dtype) * g


def layer_norm(x, g, b):
    xf = x.astype(jnp.float32)
    mu = jnp.mean(xf, axis=-1, keepdims=True)
    var = jnp.mean(jnp.square(xf - mu), axis=-1, keepdims=True)
    return ((xf - mu) * lax.rsqrt(var + EPS)).astype(x.dtype) * g + b


def alibi_slopes(n_heads):
    return jnp.exp2(-8.0 * jnp.arange(1, n_heads + 1, dtype=jnp.float32) / n_heads)


def chunked_spatial_gating(u, v, w_s, b_s, ln_g, ln_b):
    bsz, s, _ = v.shape
    v = layer_norm(v, ln_g, ln_b)
    vc = v.reshape(bsz, s // CHUNK, CHUNK, A_GROUPS, A_GROUP_DIM)
    causal = jnp.tril(jnp.ones((CHUNK, CHUNK), dtype=bool))
    ws = jnp.where(causal[None], w_s, jnp.zeros_like(w_s))
    mixed = jnp.einsum('gts,bcsgd->bctgd', ws, vc) + b_s.T[None, None, :, :, None]
    return u * mixed.reshape(bsz, s, A_WIDTH)


def dilated_window_attention(q, k, v, slopes, window, dilation):
    bsz, s, h, dh = q.shape
    n_back = window // dilation
    blk = n_back
    span = dilation * blk
    sp = -(-s // span) * span
    pad = sp - s
    sub_len = sp // dilation
    nb = sub_len // blk

    def to_sub(t):
        t = jnp.pad(t, ((0, 0), (0, pad), (0, 0), (0, 0)))
        t = jnp.moveaxis(t.reshape(bsz, sub_len, dilation, h, dh), 2, 1)
        return t.reshape(bsz, dilation, nb, blk, h, dh)

    def with_prev(t):
        prev = jnp.pad(t, ((0, 0), (0, 0), (1, 0), (0, 0), (0, 0), (0, 0)))[:, :, :-1]
        return jnp.concatenate([prev, t], axis=3)

    qs = to_sub(q)
    kb = with_prev(to_sub(k))
    vb = with_prev(to_sub(v))
    scores = jnp.einsum('brnqhd,brnkhd->brnhqk', qs, kb).astype(jnp.float32)
    qi = jnp.arange(blk)[:, None]
    ki = jnp.arange(2 * blk)[None, :]
    delta = qi + blk - ki
    blk_idx = jnp.arange(nb)[:, None, None]
    valid = (delta >= 0) & (delta <= n_back) & (blk_idx * blk + qi - delta >= 0)
    bias = -slopes[:, None, None] * (dilation * delta).astype(jnp.float32)
    scores = jnp.where(valid[None, None, :, None], scores + bias[None, None, None], -jnp.inf)
    m = jnp.max(scores, axis=-1, keepdims=True)
    p = jnp.exp(scores - m)
    den = jnp.sum(p, axis=-1)
    out = jnp.einsum('brnhqk,brnkhd->brnqhd', p, vb.astype(jnp.float32))
    out = out / jnp.swapaxes(den, -1, -2)[..., None]
    lse = jnp.swapaxes(m[..., 0] + jnp.log(den), -1, -2)

    def from_sub(t):
        rest = t.shape[4:]
        t = jnp.moveaxis(t.reshape((bsz, dilation, sub_len) + rest), 1, 2)
        return t.reshape((bsz, sp) + rest)[:, :s]

    return from_sub(out).astype(q.dtype), from_sub(lse)


def dilated_attention_mixture(q, k, v):
    slopes = alibi_slopes(B_HEADS)
    outs, lses = [], []
    for window, dilation in DILATED_PATTERNS:
        o, l = dilated_window_attention(q, k, v, slopes, window, dilation)
        outs.append(o)
        lses.append(l)
    w = jax.nn.softmax(jnp.stack(lses, axis=-1), axis=-1)
    o = jnp.stack(outs, axis=-1).astype(jnp.float32)
    return jnp.einsum('bshdp,bshp->bshd', o, w).astype(q.dtype)


def setup_inputs(seed: int = 0) -> dict:
    key = jax.random.key(seed)
    ks = jax.random.split(key, 20)
    f32 = jnp.float32

    def nrm(k, shape, scale):
        return jax.random.normal(k, shape, f32) * scale

    def gain(k, n):
        return 1.0 + nrm(k, (DEPTH, n), 0.02)

    return {
        "x": nrm(ks[0], (BATCH, SEQ, D_MODEL), 1.0),
        "norm_mix_pre": gain(ks[1], D_MODEL),
        "w_in": nrm(ks[2], (DEPTH, D_MODEL, IN_COLS), D_MODEL ** -0.5),
        "b_gate": nrm(ks[3], (DEPTH, N_BRANCHES, D_MODEL), 0.02),
        "ln_v_g": gain(ks[4], A_WIDTH),
        "ln_v_b": nrm(ks[5], (DEPTH, A_WIDTH), 0.02),
        "w_s": nrm(ks[6], (DEPTH, A_GROUPS, CHUNK, CHUNK), 0.5 * CHUNK ** -0.5),
        "b_s": 1.0 + nrm(ks[7], (DEPTH, A_GROUPS, CHUNK), 0.02),
        "w_a_proj": nrm(ks[8], (DEPTH, A_WIDTH, D_MODEL), A_WIDTH ** -0.5),
        "w_b_proj": nrm(ks[9], (DEPTH, B_WIDTH, D_MODEL), B_WIDTH ** -0.5),
        "w_out": nrm(ks[10], (DEPTH, D_MODEL, D_MODEL), D_MODEL ** -0.5),
        "norm_mix_post": gain(ks[11], D_MODEL),
        "norm_ffn_pre": gain(ks[12], D_MODEL),
        "w_ff1": nrm(ks[13], (DEPTH, D_MODEL, D_FF), D_MODEL ** -0.5),
        "w_ff2": nrm(ks[14], (DEPTH, D_FF, D_MODEL), D_FF ** -0.5),
        "norm_ffn_post": gain(ks[15], D_MODEL),
    }


def reference(x, norm_mix_pre, w_in, b_gate, ln_v_g, ln_v_b, w_s, b_s, w_a_proj, w_b_proj,
              w_out, norm_mix_post, norm_ffn_pre, w_ff1, w_ff2, norm_ffn_post):
    bsz, s, _ = x.shape
    q_scale = 1.0 / math.sqrt(B_HEAD_DIM)
    for l in range(DEPTH):
        h = rms_norm(x, norm_mix_pre[l])
        z = h @ w_in[l]
        u_a, v_a, q, k, v_b, g_a, g_b = jnp.split(z, IN_SPLITS, axis=-1)
        y_a = chunked_spatial_gating(jax.nn.gelu(u_a), jax.nn.gelu(v_a), w_s[l], b_s[l],
                                     ln_v_g[l], ln_v_b[l])
        q = q.reshape(bsz, s, B_HEADS, B_HEAD_DIM) * q_scale
        k = k.reshape(bsz, s, B_HEADS, B_HEAD_DIM)
        v_b = v_b.reshape(bsz, s, B_HEADS, B_HEAD_DIM)
        y_b = dilated_attention_mixture(q, k, v_b).reshape(bsz, s, B_WIDTH)
        merged = (jax.nn.sigmoid(g_a + b_gate[l, 0]) * (y_a @ w_a_proj[l])
                  + jax.nn.sigmoid(g_b + b_gate[l, 1]) * (y_b @ w_b_proj[l]))
        x = x + rms_norm(merged @ w_out[l], norm_mix_post[l])
        h = rms_norm(x, norm_ffn_pre[l])
        f = jnp.square(jax.nn.relu(h @ w_ff1[l])) @ w_ff2[l]
        x = x + rms_norm(f, norm_ffn_post[l])
    return x
```

```python
import math
from contextlib import ExitStack
import numpy as np
import ml_dtypes
import concourse.bass as bass
import concourse.mybir as mybir
from concourse.bass_utils import run_bass_kernel_spmd

F32 = mybir.dt.float32
BF16 = mybir.dt.bfloat16
AF = mybir.ActivationFunctionType
ALU = mybir.AluOpType
EPS = 1e-6
SEQ = 2048
DM = 1024
NCORES = 8


class Sched:
    ENGS = ("pe", "act", "dve", "pool", "sp")

    def __init__(self, nc):
        self.nc = nc
        self.es = ExitStack()
        self.lists = {e: [] for e in self.ENGS}
        self.sem = {e: self.es.enter_context(nc.semaphore("s_" + e)) for e in self.ENGS}
        self.cnt = {e: 0 for e in self.ENGS}
        self.waited = {e: {} for e in self.ENGS}
        self.lastw = {}
        self.readers = {}
        self.dma_sems = {}
        self.dma_cnt = {}

    def _need(self, eng, dep):
        key, sem, val = dep
        if key == eng and eng == "pe":
            return
        w = self.waited[eng]
        if w.get(key, 0) >= val:
            return
        w[key] = val
        self.lists[eng].append(("wait", sem, val))

    def _deps(self, eng, reads, writes):
        for b in list(reads) + list(writes):
            d = self.lastw.get(b)
            if d is not None:
                self._need(eng, d)
        for b in writes:
            for d in self.readers.get(b, ()):
                self._need(eng, d)

    def _mark(self, tok, reads, writes):
        for b in writes:
            self.lastw[b] = tok
            self.readers[b] = []
        for b in reads:
            self.readers.setdefault(b, []).append(tok)

    def op(self, eng, fn, reads=(), writes=(), signal=True):
        self._deps(eng, reads, writes)
        if signal:
            self.cnt[eng] += 1
            self.lists[eng].append(("op", fn, self.sem[eng], 1))
            tok = (eng, self.sem[eng], self.cnt[eng])
        else:
            self.lists[eng].append(("op", fn, None, 0))
            tok = (eng, self.sem[eng], self.cnt[eng] + 1)
        self._mark(tok, reads, writes)

    def dma(self, queue, fn, chan, reads=(), writes=()):
        if chan not in self.dma_sems:
            self.dma_sems[chan] = self.es.enter_context(self.nc.semaphore("d_" + str(chan)))
            self.dma_cnt[chan] = 0
        self._deps(queue, reads, writes)
        self.dma_cnt[chan] += 16
        sem = self.dma_sems[chan]
        self.lists[queue].append(("op", fn, sem, 16))
        tok = ("dma_" + str(chan), sem, self.dma_cnt[chan])
        self._mark(tok, reads, writes)

    def barrier(self):
        toks = [(e, self.sem[e], self.cnt[e]) for e in ("pe", "act", "dve", "pool") if self.cnt[e] > 0]
        toks += [("dma_" + str(c), self.dma_sems[c], self.dma_cnt[c]) for c in self.dma_sems]
        for e in self.ENGS:
            for t in toks:
                if t[0] != e:
                    self._need(e, t)
                elif e != "pe":
                    self._need(e, t)

    def emit(self):
        nc = self.nc
        lists = self.lists

        def replay(lst, e):
            for it in lst:
                if it[0] == "wait":
                    e.wait_ge(it[1], it[2])
                else:
                    ins = it[1](e)
                    if it[2] is not None:
                        ins.then_inc(it[2], it[3])

        with nc.Block() as block:
            @block.tensor
            def _(e):
                replay(lists["pe"], e)

            @block.scalar
            def _(e):
                replay(lists["act"], e)

            @block.vector
            def _(e):
                replay(lists["dve"], e)

            @block.gpsimd
            def _(e):
                replay(lists["pool"], e)

            @block.sync
            def _(e):
                replay(lists["sp"], e)
        self.es.close()


def tokap(T, rows, p, blk):
    if p == 0:
        return T[rows, 128 * blk:128 * blk + 128]
    if p == 1:
        c, n = blk // 4, blk % 4
        s = 512 * n + c
        return T[rows, s:s + 509:4]
    return T[rows, blk:SEQ:16]


def build_nc():
    nc = bass.Bass("TRN2", target_bir_lowering=False)

    def din(name, shape, dt=F32):
        return nc.dram_tensor(name, list(shape), dt, kind="ExternalInput").ap()

    x_d = din("x", [SEQ, DM])
    win_d = din("w_in", [56, 128, 1024])
    wa_d = din("w_a", [8, 128, 1024])
    wb_d = din("w_b", [8, 128, 1024])
    wo_d = din("w_o", [128, 8, 1024])
    w1_d = din("w_1", [128, 8, 4096])
    w2_d = din("w_2", [16, 128, 2048])
    wsT_d = din("wsT", [128, 8, 128])
    bs_d = din("b_s", [1, 1024])
    lgrow_d = din("ln_g_row", [1, 1024])
    lb_d = din("ln_b", [128, 8])
    bg_d = din("b_gate", [128, 16])
    gains_d = din("gains", [4, 1024])
    ident_d = din("ident", [128, 128], BF16)
    identf_d = din("identf", [128, 128])
    mask_d = din("mask", [128, 128])
    ones_d = din("onesf", [128, 128])
    eb_d = din("eb", [128, 80, 128], BF16)
    out_d = nc.dram_tensor("out", [SEQ, DM], F32, kind="ExternalOutput").ap()
    den_d = nc.dram_tensor("den_scr", [16, SEQ], F32).ap()
    w1s_d = nc.dram_tensor("w1_bf", [8, 128, 4096], BF16).ap()
    w2s_d = nc.dram_tensor("w2_bf", [8, 128, 4096], BF16).ap()

    S = Sched(nc)
    top = ExitStack()

    def sb(stack, name, shape, dt):
        return stack.enter_context(nc.sbuf_tensor("sb_" + name, list(shape), dt))

    ps = [top.enter_context(nc.psum_tensor(f"pst{k}", [128, 512], F32)) for k in range(8)]
    PK = [f"ps{k}" for k in range(8)]
    ident = sb(top, "ident", [128, 128], BF16)
    identf = sb(top, "identf", [128, 128], F32)
    hT = sb(top, "hT", [128, 8, SEQ], BF16)
    yaT = sb(top, "yaT", [128, 8, SEQ], BF16)
    big1 = sb(top, "big1", [128, 16, 1024], BF16)
    gc = big1
    ybT = big1[:].rearrange("p a b -> p (a b)").rearrange("p (k t) -> p k t", k=8)
    stat = sb(top, "stat", [128, 64], F32)
    ssq = stat[:, 0:16]
    rstd = stat[:, 16:32]
    varall = stat[:, 32:48]
    rstdv = stat[:, 48:64]

    cvn = [0]

    pend = []

    def conv_step(stg):
        while pend:
            pend.pop(0)()
        if conv:
            conv.pop(0)(stg)

    def conv_w1(fq, h):
        def f(stg):
            b = cvn[0] % 2
            cvn[0] += 1
            S.dma("pool", lambda e: e.dma_start(out=stg[b][:].rearrange("p (k c) -> p k c", k=4),
                                                in_=w1_d[:, 4 * h:4 * h + 4, 512 * fq:512 * fq + 512]), f"stgl{b}", writes=[f"stg{b}"])
            pend.append(lambda: S.dma("sp", lambda e: e.dma_start(out=w1s_d[fq][:, 2048 * h:2048 * h + 2048], in_=stg[b][:]),
                                      f"stgs{b}", reads=[f"stg{b}"], writes=["w1s"]))
        return f

    def conv_w2(cb, h2):
        def f(stg):
            b = cvn[0] % 2
            cvn[0] += 1
            S.dma("pool", lambda e: e.dma_start(out=stg[b][:], in_=w2_d[2 * cb + h2]), f"stgl{b}", writes=[f"stg{b}"])
            pend.append(lambda: S.dma("sp", lambda e: e.dma_start(out=w2s_d[cb][:, 2048 * h2:2048 * h2 + 2048], in_=stg[b][:]),
                                      f"stgs{b}", reads=[f"stg{b}"], writes=["w2s"]))
        return f
    conv = [conv_w1(fq, h) for fq in range(8) for h in range(2)] + [conv_w2(cb, h2) for cb in range(8) for h2 in range(2)]
    S.dma("sp", lambda e: e.dma_start(out=ident[:], in_=ident_d[:, :]), "c0a", writes=["ident"])
    S.dma("sp", lambda e: e.dma_start(out=identf[:], in_=identf_d[:, :]), "c0b", writes=["identf"])

    def mm(out, lhsT, rhs, start, stop, reads, writes, signal):
        S.op("pe", lambda e: e.matmul(out, lhsT=lhsT, rhs=rhs, start=start, stop=stop, skip_group_check=True),
             reads=reads, writes=writes, signal=signal)

    def rms_rstd(col, n_in=1.0 / DM):
        S.op("act", lambda e: e.activation(out=col, in_=col, func=AF.Ln, scale=n_in, bias=EPS),
             reads=["stat"], writes=["stat"])
        S.op("act", lambda e: e.activation(out=col, in_=col, func=AF.Exp, scale=-0.5),
             reads=["stat"], writes=["stat"])

    with ExitStack() as ph:
        gbc = sb(ph, "gbc", [128, DM], F32)
        xt = [sb(ph, f"xt{i}", [128, DM], F32) for i in range(3)]
        xn = [sb(ph, f"xn{i}", [128, DM], BF16) for i in range(3)]
        junk = sb(ph, "junk", [128, DM], BF16)
        wv = sb(ph, "wv", [128, 8, 1024], BF16)
        gv = sb(ph, "gv", [128, DM], F32)
        bst = sb(ph, "bst", [128, 16], F32)
        wua = sb(ph, "wua", [128, 8, 1024], BF16)
        wsTf = sb(ph, "wsTf", [128, 8, 128], F32)
        maskt = sb(ph, "maskt", [128, 128], F32)
        onesf = sb(ph, "onesf", [128, 128], F32)
        BS = sb(ph, "BS", [128, 8, 128], F32)
        R = sb(ph, "R", [128, 8, 128], F32)
        lb = sb(ph, "lb", [128, 8], F32)
        wsn = [sb(ph, f"wsn{i}", [128, 8, 128], BF16) for i in range(2)]
        tmpm = [sb(ph, f"tmpm{i}", [128, 4, 128], BF16) for i in range(2)]
        lgbc = sb(ph, "lgbc", [128, DM], F32)
        S.dma("sp", lambda e: e.dma_start(out=lgbc[:], in_=lgrow_d[0:1, :].partition_broadcast(128)), "c1b", writes=["lgbc"])
        S.dma("sp", lambda e: e.dma_start(out=gbc[:], in_=gains_d[0:1, :].partition_broadcast(128)), "c1", writes=["gbc"])
        def wblk(d, blk):
            return d[blk].rearrange("p (k c) -> p k c", k=8)
        for j8 in range(8):
            S.dma("pool", lambda e, j8=j8: e.dma_start(out=wv[:, :, 128 * j8:128 * j8 + 128], in_=wblk(win_d, 8 + j8)),
                  "wv", writes=["wv"])
        def p1A(i):
            b = i % 3
            S.dma("sp", lambda e: e.dma_start(out=xt[b][:], in_=x_d[128 * i:128 * i + 128, :]), f"x{b}", writes=[f"xt{b}"])
            S.op("act", lambda e: e.activation(out=junk[:], in_=xt[b][:], func=AF.Square, accum_out=ssq[:, i:i + 1]),
                 reads=[f"xt{b}"], writes=["junk", f"ssq{i}"])
            S.op("act", lambda e: e.activation(out=rstd[:, i:i + 1], in_=ssq[:, i:i + 1], func=AF.Ln, scale=1.0 / DM, bias=EPS),
                 reads=[f"ssq{i}"], writes=[f"rstd{i}"])
            S.op("act", lambda e: e.activation(out=rstd[:, i:i + 1], in_=rstd[:, i:i + 1], func=AF.Exp, scale=-0.5),
                 reads=[], writes=[f"rstd{i}"])
            S.op("dve", lambda e: e.scalar_tensor_tensor(out=xn[b][:], in0=xt[b][:], scalar=rstd[:, i:i + 1],
                                                         in1=gbc[:], op0=ALU.mult, op1=ALU.mult),
                 reads=[f"xt{b}", f"rstd{i}", "gbc"], writes=[f"xn{b}"])

        def p1B(i):
            b = i % 3
            pk = i % 2
            pbf = ps[pk][:].bitcast(BF16)
            for kc in range(8):
                S.op("pe", lambda e, kc=kc: e.transpose(out=pbf[:, kc * 128:(kc + 1) * 128],
                                                        in_=xn[b][:, kc * 128:(kc + 1) * 128], identity=ident[:]),
                     reads=[f"xn{b}", "ident"], writes=[PK[pk]], signal=(kc == 7))
            S.op("dve", lambda e: e.tensor_copy(out=hT[:, :, 128 * i:128 * i + 128], in_=pbf.rearrange("p (k t) -> p k t", k=8)),
                 reads=[], writes=[PK[pk], f"hT{i}"])

        S.dma("sp", lambda e: e.dma_start(out=wsTf[:], in_=wsT_d[:, :, :]), "c2a", writes=["wsTf"])
        S.dma("sp", lambda e: e.dma_start(out=maskt[:], in_=mask_d[:, :]), "c2b", writes=["maskt"])
        S.dma("sp", lambda e: e.dma_start(out=onesf[:], in_=ones_d[:, :]), "c2c", writes=["onesf"])
        S.dma("sp", lambda e: e.dma_start(out=BS[:].rearrange("p g t -> p (g t)"), in_=bs_d[0:1, :].partition_broadcast(128)),
              "c2d", writes=["BS"])
        S.dma("sp", lambda e: e.dma_start(out=lb[:], in_=lb_d[:, :]), "c2f", writes=["lb"])
        for i in range(18):
            if i < 16:
                p1A(i)
            if i >= 2:
                p1B(i - 2)
        for j8 in range(8):
            S.dma("pool", lambda e, j8=j8: e.dma_start(out=wua[:, :, 128 * j8:128 * j8 + 128], in_=wblk(win_d, j8)),
                  "wua", reads=["hT9"], writes=["wua"])
        S.op("dve", lambda e: e.tensor_tensor(out=wsTf[:], in0=wsTf[:], in1=maskt[:].unsqueeze(1).broadcast_to([128, 8, 128]),
                                              op=ALU.mult), reads=["wsTf", "maskt"], writes=["wsTf"])
        for hf in range(2):
            mm(ps[4 + hf][:, :], onesf[:], wsTf[:, 4 * hf:4 * hf + 4, :].rearrange("p g t -> p (g t)"), True, True,
               ["onesf", "wsTf"], [PK[4 + hf]], True)
            for gi in range(4):
                g = 4 * hf + gi
                S.op("dve", lambda e, g=g, gi=gi, hf=hf: e.scalar_tensor_tensor(
                    out=R[:, g, :], in0=ps[4 + hf][:, 128 * gi:128 * gi + 128], scalar=lb[:, g:g + 1], in1=BS[:, g, :],
                    op0=ALU.mult, op1=ALU.add), reads=["lb", "BS"], writes=[PK[4 + hf], "R"])
        HT = [f"hT{i}" for i in range(16)]

        for n in range(16):
            pk = 2 * (n % 2)
            for hf in range(2):
                for kc in range(8):
                    mm(ps[pk + hf][:, :], hT[:, kc, 128 * n:128 * n + 128], wv[:, kc, 512 * hf:512 * hf + 512],
                       kc == 0, kc == 7, [f"hT{n}", "wv"], [PK[pk + hf]], kc == 7)
                S.op("act", lambda e, hf=hf, pk=pk: e.activation(out=gv[:, 512 * hf:512 * hf + 512], in_=ps[pk + hf][:, :],
                                                                 func=AF.Gelu_apprx_tanh),
                     reads=[], writes=[PK[pk + hf], f"gv{hf}"])
                S.op("dve", lambda e, hf=hf: e.bn_stats(out=bst[:, 6 * hf:6 * hf + 6], in_=gv[:, 512 * hf:512 * hf + 512]),
                     reads=[f"gv{hf}"], writes=["bst"])
            S.op("dve", lambda e: e.bn_aggr(out=bst[:, 12:14], in_=bst[:, 0:12]), reads=["bst"], writes=["bst"])
            S.op("dve", lambda e, n=n: e.scalar_tensor_tensor(out=gc[:, n, :], in0=gv[:], scalar=bst[:, 12:13], in1=lgbc[:],
                                                              op0=ALU.subtract, op1=ALU.mult),
                 reads=["gv0", "gv1", "bst", "lgbc"], writes=[f"gc{n}"])
            S.op("dve", lambda e, n=n: e.tensor_copy(out=varall[:, n:n + 1], in_=bst[:, 13:14]),
                 reads=["bst"], writes=["stat"])

        S.op("act", lambda e: e.activation(out=rstdv, in_=varall, func=AF.Ln, scale=1.0, bias=EPS),
             reads=["stat"], writes=["stat"])
        S.op("act", lambda e: e.activation(out=rstdv, in_=rstdv, func=AF.Exp, scale=-0.5),
             reads=["stat"], writes=["stat"])

        def mkwsn(n):
            b = n % 2
            S.op("dve", lambda e: e.tensor_scalar(out=wsn[b][:], in0=wsTf[:], scalar1=rstdv[:, n:n + 1], scalar2=1.0,
                                                  op0=ALU.mult, op1=ALU.mult),
                 reads=["wsTf", "stat"], writes=[f"wsn{b}"])

        def gate_mm(n):
            b = n % 2
            for hf in range(2):
                pk = (2 * n + hf) % 4
                for gi in range(4):
                    g = 4 * hf + gi
                    mm(ps[pk][:, 128 * gi:128 * gi + 128], gc[:, n, 128 * g:128 * g + 128], wsn[b][:, g, :], gi == 0, True,
                       [f"gc{n}", f"wsn{b}"], [PK[pk]], gi == 3)

        def gate_ev(n):
            for hf in range(2):
                pk = (2 * n + hf) % 4
                tb = (2 * n + hf) % 2
                S.op("dve", lambda e, hf=hf, pk=pk, tb=tb: e.tensor_tensor(
                    out=tmpm[tb][:].rearrange("p a t -> p (a t)"), in0=ps[pk][:, :],
                    in1=R[:, 4 * hf:4 * hf + 4, :].rearrange("p a t -> p (a t)"), op=ALU.add),
                    reads=["R"], writes=[PK[pk], f"tmpm{tb}"])
                yv = yaT[:, 4 * hf:4 * hf + 4, 128 * n:128 * n + 128]
                S.op("pool" if hf == 1 else "dve", lambda e, yv=yv, tb=tb: e.tensor_tensor(out=yv, in0=yv, in1=tmpm[tb][:], op=ALU.mult),
                     reads=[f"tmpm{tb}"], writes=[f"ya{c}_{n // 4}" for c in range(4 * hf, 4 * hf + 4)])

        gsteps = []
        def add_gate_window(w):
            for n in range(4 * w, 4 * w + 4):
                gsteps.append(lambda n=n: (mkwsn(n), gate_mm(n)))
                gsteps.append(lambda n=n: gate_ev(n))

        for w in range(4):
            for cb in range(8):
                pk = 4 + (w * 8 + cb) % 4
                for kc in range(8):
                    mm(ps[pk][:, :], wua[:, kc, 128 * cb:128 * cb + 128], hT[:, kc, 512 * w:512 * w + 512], kc == 0, kc == 7,
                       ["wua"] + HT[4 * w:4 * w + 4], [PK[pk]], kc == 7)
                S.op("act", lambda e, cb=cb, w=w, pk=pk: e.activation(out=yaT[:, cb, 512 * w:512 * w + 512], in_=ps[pk][:, :],
                                                                      func=AF.Gelu_apprx_tanh),
                     reads=[], writes=[PK[pk], f"ya{cb}_{w}"])
                if gsteps:
                    gsteps.pop(0)()
            add_gate_window(w)
        while gsteps:
            gsteps.pop(0)()
        S.barrier()

    YA = [f"ya{c}_{w}" for c in range(8) for w in range(4)]
    with ExitStack() as ph:
        EBh = [sb(ph, f"EBh{i}", [128, 10, 128], BF16) for i in range(2)]
        QZ = [[sb(ph, f"QZ{i}_{hh}", [128, SEQ], BF16) for hh in range(2)] for i in range(2)]
        KT = [sb(ph, f"KT{i}", [128, SEQ], BF16) for i in range(2)]
        VT = sb(ph, "VT", [128, SEQ], BF16)
        vaug = [sb(ph, f"vaug{i}", [128, 48, 2, 65], BF16) for i in range(2)]
        NET, NPT, LAG = 2, 7, 3
        stg = [sb(ph, f"stg{i}", [128, 2048], BF16) for i in range(2)]
        Et = [sb(ph, f"Et{i}", [128, 512], BF16) for i in range(NET)]
        PT = [sb(ph, f"PT{i}", [128, 512], BF16) for i in range(NPT)]
        wqkv = [sb(ph, f"wqkv{i}", [128, 8, 384], BF16) for i in range(2)]
        norm_base = nc.sbuf_base
        accs = [sb(ph, f"acc{i}", [128, SEQ], F32) for i in range(2)]
        denbc = sb(ph, "denbc", [64, 1024], F32)
        ytmp = sb(ph, "ytmp", [64, 1024], BF16)
        dsm = sb(ph, "dsm", [16, 128], F32)
        for i in range(2):
            S.op("dve", lambda e, i=i: e.memset(vaug[i][:].rearrange("p a b c -> p (a b c)"), 1.0), reads=[], writes=[f"vaug{i}"])
            S.op("dve", lambda e, i=i: e.memset(QZ[i][0][64:128, :], 0.0), reads=[], writes=[f"QZ{i}_0"])
            S.op("dve", lambda e, i=i: e.memset(QZ[i][1][0:64, :], 0.0), reads=[], writes=[f"QZ{i}_1"])
        SB_, AB_ = [2, 3, 4], [5, 6, 7]

        def tokapN(T, p, kb, nblk):
            if p == 0:
                return T[:, 128 * kb:128 * kb + 128 * nblk]
            if p == 1:
                c, n = kb // 4, kb % 4
                s = 512 * n + c
                return T[:, s:s + 4 * (128 * nblk - 1) + 1:4]
            return T[:, kb:SEQ:16]

        sctr = [0]
        actr = [0]
        pctr = [0]

        def prep_items(hp):
            d = hp % 2
            items = []

            def load():
                S.dma("sp", lambda e: e.dma_start(out=EBh[d][:], in_=eb_d[:, 10 * hp:10 * hp + 10, :]), f"ebh{d}", writes=[f"EBh{d}"])
                for j in range(3):
                    S.dma("pool", lambda e, j=j: e.dma_start(out=wqkv[d][:, :, 128 * j:128 * j + 128], in_=wblk(win_d, 16 + 8 * j + hp)),
                          f"wqkv{d}", writes=[f"wqkv{d}"])
            items.append(load)

            def proj(j, w):
                def f():
                    pk = pctr[0] % 2
                    pctr[0] += 1
                    for kc in range(8):
                        mm(ps[pk][:, :], wqkv[d][:, kc, 128 * j:128 * j + 128], hT[:, kc, 512 * w:512 * w + 512], kc == 0, kc == 7,
                           [f"wqkv{d}"] + HT[4 * w:4 * w + 4], [PK[pk]], kc == 7)
                    if j == 0:
                        for hh in range(2):
                            S.op("act", lambda e, hh=hh: e.activation(
                                out=QZ[d][hh][64 * hh:64 * hh + 64, 512 * w:512 * w + 512], in_=ps[pk][64 * hh:64 * hh + 64, :],
                                func=AF.Copy, scale=0.125), reads=[], writes=[PK[pk], f"QZ{d}_{hh}"])
                    elif j == 1:
                        S.op("act", lambda e: e.activation(out=KT[d][:, 512 * w:512 * w + 512], in_=ps[pk][:, :], func=AF.Copy),
                             reads=[], writes=[PK[pk], f"KT{d}"])
                    else:
                        S.op("act", lambda e: e.activation(out=VT[:, 512 * w:512 * w + 512], in_=ps[pk][:, :], func=AF.Copy),
                             reads=[], writes=[PK[pk], "VT"])
                return f
            for j in (2, 1, 0):
                for w in range(4):
                    items.append(proj(j, w))

            def vtr(p, half):
                def f():
                    pk = pctr[0] % 2
                    pctr[0] += 1
                    pbf = ps[pk][:].bitcast(BF16)
                    for bi in range(8):
                        blk = 8 * half + bi
                        S.op("pe", lambda e, blk=blk, bi=bi: e.transpose(
                            out=pbf[:, 128 * bi:128 * bi + 128], in_=tokap(VT, slice(0, 128), p, blk), identity=ident[:]),
                            reads=["VT", "ident"], writes=[PK[pk]], signal=(bi == 7))
                    b0 = 16 * p + 8 * half
                    S.op("act", lambda e: e.activation(
                        out=vaug[d][:, b0:b0 + 8, :, 0:64], in_=pbf.rearrange("p (a h d) -> p a h d", a=8, h=2), func=AF.Copy),
                        reads=[], writes=[PK[pk], f"vaug{d}"])
                return f
            its = items[:5] + items[5:9] + [vtr(p, half) for p in range(3) for half in range(2)] + items[9:]
            return its

        deferred = []
        gs = [0]

        def run_deferred(force=False):
            while deferred and (force or deferred[0][0] <= gs[0]):
                deferred.pop(0)[1]()

        for it in prep_items(0):
            it()
        for hp in range(8):
            d = hp % 2
            nxt = prep_items(hp + 1) if hp < 7 else []
            sjobs, pvjobs = [], []
            for hh in range(2):
                eb0 = 5 * hh
                base = len(sjobs)
                for j in range(8):
                    sjobs.append(dict(hh=hh, p=0, kbs=[(2 * j, 2), (2 * j + 1, 2 if 2 * j + 1 < 15 else 1)], ebi=eb0))
                for w in range(4):
                    parts = []
                    if w > 0:
                        kb = 4 * w - 1
                        parts.append((base + kb // 2, 384, 128, kb, 0))
                    for a in range(4):
                        kb = 4 * w + a
                        parts.append((base + kb // 2, 0 if kb % 2 == 0 else 256, 256 if a < 3 else 128, kb, 128 * a))
                    pvjobs.append(dict(hh=hh, p=0, grp=w, parts=parts, last=False))
                for c in range(4):
                    base2 = len(sjobs)
                    sjobs.append(dict(hh=hh, p=1, kbs=[(4 * c, 2), (4 * c + 1, 2)], ebi=eb0 + 2))
                    sjobs.append(dict(hh=hh, p=1, kbs=[(4 * c + 2, 2), (4 * c + 3, 1)], ebi=eb0 + 2))
                    parts = [(base2, 0, 256, 4 * c, 0), (base2, 256, 256, 4 * c + 1, 128),
                             (base2 + 1, 0, 256, 4 * c + 2, 256), (base2 + 1, 256, 128, 4 * c + 3, 384)]
                    pvjobs.append(dict(hh=hh, p=1, grp=c, parts=parts, last=False))
                for g in range(4):
                    base3 = len(sjobs)
                    sjobs.append(dict(hh=hh, p=2, kbs=[(4 * g + a, 1) for a in range(4)], ebi=eb0 + 4))
                    parts = [(base3, 128 * a, 128, 4 * g + a, 128 * a) for a in range(4)]
                    pvjobs.append(dict(hh=hh, p=2, grp=g, parts=parts, last=(g == 3)))

            def emit_s(si):
                sj = sjobs[si]
                p, hh, ebi = sj["p"], sj["hh"], sj["ebi"]
                EB = EBh[d]
                bank = SB_[sctr[0] % 3]
                sctr[0] += 1
                es, ptb = si % NET, si % NPT
                cw = 256 if p < 2 else 128
                nk = len(sj["kbs"])
                for i, (kb, nblk) in enumerate(sj["kbs"]):
                    mm(ps[bank][:, cw * i:cw * i + 128 * nblk], tokap(KT[d], slice(0, 128), p, kb), tokapN(QZ[d][hh], p, kb, nblk),
                       i == 0, True, [f"KT{d}", f"QZ{d}_{hh}"], [PK[bank]], i == nk - 1)
                W = cw * (nk - 1) + 128 * sj["kbs"][-1][1]
                S.op("act", lambda e: e.activation(out=Et[es][:, 0:W], in_=ps[bank][:, 0:W], func=AF.Exp),
                     reads=[], writes=[PK[bank], f"Et{es}"])
                rk = [f"Et{es}", f"EBh{d}"]
                if p == 2:
                    S.op("dve", lambda e: e.tensor_tensor(
                        out=PT[ptb][:, :].rearrange("p (a q) -> p a q", q=128), in0=Et[es][:, :].rearrange("p (a q) -> p a q", q=128),
                        in1=EB[:, ebi:ebi + 1, :].broadcast_to([128, 4, 128]), op=ALU.mult), reads=rk, writes=[f"PT{ptb}"])
                else:
                    eb2 = EB[:, ebi:ebi + 2, :].rearrange("p a q -> p (a q)")
                    if sj["kbs"][-1][1] == 2:
                        S.op("dve", lambda e: e.tensor_tensor(
                            out=PT[ptb][:, :].rearrange("p (a q) -> p a q", q=256), in0=Et[es][:, :].rearrange("p (a q) -> p a q", q=256),
                            in1=eb2.unsqueeze(1).broadcast_to([128, 2, 256]), op=ALU.mult), reads=rk, writes=[f"PT{ptb}"])
                    else:
                        S.op("dve", lambda e: e.tensor_tensor(out=PT[ptb][:, 0:256], in0=Et[es][:, 0:256], in1=eb2, op=ALU.mult),
                             reads=rk, writes=[f"PT{ptb}"])
                        S.op("dve", lambda e: e.tensor_tensor(out=PT[ptb][:, 256:384], in0=Et[es][:, 256:384], in1=EB[:, ebi, :],
                                                              op=ALU.mult), reads=rk, writes=[f"PT{ptb}"])

            def emit_pv(pj, hp=hp, d=d):
                p, hh, grp = pj["p"], pj["hh"], pj["grp"]
                h = 2 * hp + hh
                acc = accs[hh]
                ak = f"acc{hh}"
                bank = AB_[actr[0] % 3]
                actr[0] += 1
                np_ = len(pj["parts"])
                for i, (sidx, ptcol, n, vblk, acol) in enumerate(pj["parts"]):
                    mm(ps[bank][:65, acol:acol + n], vaug[d][:, 16 * p + vblk, hh, :], PT[sidx % NPT][:, ptcol:ptcol + n],
                       i == 0, True, [f"vaug{d}", f"PT{sidx % NPT}"], [PK[bank]], i == np_ - 1)
                if p == 0:
                    S.op("act", lambda e: e.activation(out=acc[0:65, 512 * grp:512 * grp + 512], in_=ps[bank][0:65, :], func=AF.Copy),
                         reads=[], writes=[PK[bank], ak])
                elif p == 1:
                    av = acc[0:65, :].rearrange("p (n i c) -> p c n i", n=4, c=4)[:, grp]
                    S.op("dve", lambda e: e.tensor_tensor(out=av, in0=ps[bank][0:65, :].rearrange("p (n i) -> p n i", n=4), in1=av,
                                                          op=ALU.add), reads=[], writes=[PK[bank], ak])
                else:
                    av = acc[0:65, :].rearrange("p (i r) -> p r i", r=16)[:, 4 * grp:4 * grp + 4, :]
                    S.op("dve", lambda e: e.tensor_tensor(out=av, in0=ps[bank][0:65, :].rearrange("p (r i) -> p r i", r=4), in1=av,
                                                          op=ALU.add), reads=[], writes=[PK[bank], ak])
                if pj["last"]:
                    def norm():
                        S.dma("sp", lambda e: e.dma_start(out=den_d[h:h + 1, :], in_=acc[64:65, :]), "den", reads=[ak], writes=["dend"])
                        S.dma("sp", lambda e: e.dma_start(out=dsm[:], in_=den_d[h].rearrange("(a b) -> a b", a=16)), "den",
                              reads=["dend"], writes=["dsm"])

                    def norm2():
                        S.op("act", lambda e: e.activation(out=dsm[:], in_=dsm[:], func=AF.Ln), reads=[], writes=["dsm"])
                        S.op("act", lambda e: e.activation(out=dsm[:], in_=dsm[:], func=AF.Exp, scale=-1.0), reads=[], writes=["dsm"])
                        S.dma("sp", lambda e: e.dma_start(out=den_d[h].rearrange("(a b) -> a b", a=16), in_=dsm[:]), "den",
                              reads=["dsm", "dend"], writes=["dend"])

                    def fin(hf):
                        def f():
                            c0 = 1024 * hf
                            S.dma("sp", lambda e: e.dma_start(out=denbc[:], in_=den_d[h:h + 1, c0:c0 + 1024].partition_broadcast(64)),
                                  "den", reads=["dend", "denbc"], writes=["denbc"])
                            if hh == 0:
                                S.op("dve", lambda e: e.tensor_tensor(out=ybT[0:64, hp, c0:c0 + 1024], in0=acc[0:64, c0:c0 + 1024],
                                                                      in1=denbc[:], op=ALU.mult),
                                     reads=[ak, "denbc"], writes=[f"yb{hp}"])
                            else:
                                S.op("dve", lambda e: e.tensor_tensor(out=ytmp[:], in0=acc[0:64, c0:c0 + 1024], in1=denbc[:], op=ALU.mult),
                                     reads=[ak, "denbc"], writes=["ytmp"])
                                S.dma("sp", lambda e: e.dma_start(out=ybT[64:128, hp, c0:c0 + 1024], in_=ytmp[:]), "ymv",
                                      reads=["ytmp"], writes=[f"yb{hp}"])
                        return f
                    deferred.append((gs[0] + 1, norm))
                    deferred.append((gs[0] + 5, norm2))
                    deferred.append((gs[0] + 10, fin(0)))
                    deferred.append((gs[0] + 13, fin(1)))

            ptr = 0
            ni = 0
            for si in range(len(sjobs)):
                emit_s(si)
                gs[0] += 1
                while ptr < len(pvjobs) and max(x[0] for x in pvjobs[ptr]["parts"]) <= si - LAG:
                    emit_pv(pvjobs[ptr])
                    ptr += 1
                run_deferred()
                if si % 2 == 1 and ni < len(nxt):
                    nxt[ni]()
                    ni += 1
                if si % 10 == 5 and (conv or pend):
                    conv_step(stg)
            while ptr < len(pvjobs):
                emit_pv(pvjobs[ptr])
                ptr += 1
            while ni < len(nxt):
                nxt[ni]()
                ni += 1
        while conv or pend:
            conv_step(stg)
        S.barrier()
        run_deferred(force=True)

    YB = [f"yb{hp}" for hp in range(8)]
    late = ExitStack()
    merged = sb(late, "merged", [128, 8, SEQ], BF16)
    wout = sb(late, "wout", [128, 8, 1024], BF16)
    gb3 = sb(late, "gb3", [128, 3, DM], F32)
    for k in range(3):
        S.dma("sp", lambda e, k=k: e.dma_start(out=gb3[:, k, :], in_=gains_d[k + 1:k + 2, :].partition_broadcast(128)),
              "c5", writes=["gb3"])
    with ExitStack() as ph:
        wg = [sb(ph, f"wg{i}", [128, 8, 512], BF16) for i in range(2)]
        bg = sb(ph, "bg", [128, 16], F32)
        sg = [sb(ph, f"sg{i}", [128, 512], F32) for i in range(2)]
        tA = sb(ph, "tA", [128, 512], F32)
        assert nc.sbuf_base <= norm_base, (nc.sbuf_base, norm_base)
        S.dma("sp", lambda e: e.dma_start(out=bg[:], in_=bg_d[:, :]), "c4", writes=["bg"])
        S.op("dve", lambda e: e.tensor_scalar(out=bg[:], in0=bg[:], scalar1=-1.0, scalar2=0.0, op0=ALU.mult, op1=ALU.add),
             reads=["bg"], writes=["bg"])
        for cb in range(8):
            b = cb % 2
            srcs = [wblk(win_d, 40 + cb), wblk(win_d, 48 + cb), wblk(wa_d, cb), wblk(wb_d, cb)]
            for j in range(4):
                S.dma("pool", lambda e, j=j, b=b, src=srcs[j]: e.dma_start(out=wg[b][:, :, 128 * j:128 * j + 128], in_=src),
                      f"wg{b}", writes=[f"wg{b}"])
            if cb == 1:
                for hf in range(2):
                    S.dma("pool", lambda e, hf=hf: e.dma_start(out=wout[:, :, 512 * hf:512 * hf + 512],
                                                               in_=wo_d[:, :, 512 * hf:512 * hf + 512]), "wout", writes=["wout"])
            for w in range(4):
                acts = [hT, hT, yaT, ybT]
                rd = [HT[4 * w:4 * w + 4], HT[4 * w:4 * w + 4], YA, YB]
                po = 4 * ((cb * 4 + w) % 2)
                for j in range(4):
                    pk = po + j
                    for kc in range(8):
                        mm(ps[pk][:, :], wg[b][:, kc, 128 * j:128 * j + 128], acts[j][:, kc, 512 * w:512 * w + 512],
                           kc == 0, kc == 7, [f"wg{b}"] + rd[j], [PK[pk]], kc == 7)
                for j in range(2):
                    S.op("act", lambda e, j=j, cb=cb, po=po: e.activation(out=sg[j][:], in_=ps[po + j][:, :], func=AF.Exp, scale=-1.0,
                                                                   bias=bg[:, 8 * j + cb:8 * j + cb + 1]),
                         reads=["bg"], writes=[PK[po + j], f"sg{j}"])
                    S.op("act", lambda e, j=j: e.activation(out=sg[j][:], in_=sg[j][:], func=AF.Ln, scale=1.0, bias=1.0),
                         reads=[f"sg{j}"], writes=[f"sg{j}"])
                    S.op("act", lambda e, j=j: e.activation(out=sg[j][:], in_=sg[j][:], func=AF.Exp, scale=-1.0),
                         reads=[f"sg{j}"], writes=[f"sg{j}"])
                S.op("dve", lambda e, po=po: e.tensor_tensor(out=tA[:], in0=ps[po + 2][:, :], in1=sg[0][:], op=ALU.mult),
                     reads=["sg0"], writes=[PK[po + 2], "tA"])
                S.op("dve", lambda e, po=po: e.tensor_tensor(out=sg[1][:], in0=ps[po + 3][:, :], in1=sg[1][:], op=ALU.mult),
                     reads=["sg1"], writes=[PK[po + 3], "sg1"])
                S.op("dve", lambda e, cb=cb, w=w: e.tensor_tensor(out=merged[:, cb, 512 * w:512 * w + 512], in0=tA[:], in1=sg[1][:],
                                                                  op=ALU.add),
                     reads=["tA", "sg1"], writes=[f"mg{w}"])
        S.barrier()

    hid = hT[:].rearrange("p k t -> p (k t)").rearrange("p (f t) -> p f t", f=32)
    yaf = yaT[:].rearrange("p k t -> p (k t)").bitcast(F32)
    fT = yaf[:, 4096:8192].rearrange("p (c t) -> p c t", c=8)
    b1f = big1[:].rearrange("p a b -> p (a b)")
    w1b = [b1f[:, 8192 + 4096 * i:8192 + 4096 * (i + 1)].rearrange("p (k c) -> p k c", k=8) for i in range(2)]
    with ExitStack() as ph:
        xo = [sb(ph, f"xo{i}", [128, DM], F32) for i in range(2)]
        x1b = b1f[:, 0:8192].bitcast(F32).rearrange("p (t d) -> p t d", t=4)
        x1s = [yaf[:, 0:4096].rearrange("p (t d) -> p t d", t=4), x1b]
        t1 = sb(ph, "t1", [128, DM], F32)
        xn3 = [sb(ph, f"xn3{i}", [128, DM], BF16) for i in range(4)]
        junk3 = sb(ph, "junk3", [128, 512], BF16)
        h2T = sb(ph, "h2T", [128, 8, 512], BF16)
        w2b = [sb(ph, f"w2b{i}", [128, 32, 128], BF16) for i in range(2)]
        st3 = sb(ph, "st3", [128, 16], F32)
        xoctr = [0]

        def post_norm(pk0, gidx, addin, addkey, outap, outkey, scol):
            sk = f"st3_{scol}"
            for hf in range(2):
                S.op("act", lambda e, hf=hf: e.activation(out=junk3[:], in_=ps[pk0 + hf][:, :], func=AF.Square,
                                                          accum_out=st3[:, scol + hf:scol + hf + 1]),
                     reads=[], writes=[PK[pk0 + hf], "junk3", sk])
            S.op("dve", lambda e: e.tensor_tensor(out=st3[:, scol:scol + 1], in0=st3[:, scol:scol + 1], in1=st3[:, scol + 1:scol + 2],
                                                  op=ALU.add), reads=[], writes=[sk])
            S.op("act", lambda e: e.activation(out=st3[:, scol:scol + 1], in_=st3[:, scol:scol + 1], func=AF.Ln, scale=1.0 / DM, bias=EPS),
                 reads=[], writes=[sk])
            S.op("act", lambda e: e.activation(out=st3[:, scol:scol + 1], in_=st3[:, scol:scol + 1], func=AF.Exp, scale=-0.5),
                 reads=[], writes=[sk])
            for hf in range(2):
                S.op("dve", lambda e, hf=hf: e.scalar_tensor_tensor(
                    out=t1[:, 512 * hf:512 * hf + 512], in0=ps[pk0 + hf][:, :], scalar=st3[:, scol:scol + 1],
                    in1=gb3[:, gidx, 512 * hf:512 * hf + 512], op0=ALU.mult, op1=ALU.mult),
                    reads=[sk, "gb3"], writes=[PK[pk0 + hf], "t1"])
            S.op("dve", lambda e: e.tensor_tensor(out=outap, in0=t1[:], in1=addin, op=ALU.add),
                 reads=["t1", addkey], writes=[outkey])

        def front_mm(w, ti):
            i = 4 * w + ti
            x1 = x1s[w % 2]
            b = xoctr[0] % 2
            xoctr[0] += 1
            S.dma("sp", lambda e: e.dma_start(out=xo[b][:], in_=x_d[128 * i:128 * i + 128, :]), f"xo{b}", writes=[f"xo{b}"])
            pk0 = 2 * (ti % 2)
            for hf in range(2):
                for kc in range(8):
                    mm(ps[pk0 + hf][:, :], merged[:, kc, 128 * i:128 * i + 128], wout[:, kc, 512 * hf:512 * hf + 512],
                       kc == 0, kc == 7, [f"mg{w}", "wout"], [PK[pk0 + hf]], kc == 7)
            x1k = f"x1_{w % 2}_{ti}"
            post_norm(pk0, 0, xo[b][:], f"xo{b}", x1[:, ti, :], x1k, 4 * (ti % 2))
            sc = 8 + ti
            S.op("act", lambda e: e.activation(out=xn3[ti][:], in_=x1[:, ti, :], func=AF.Square, accum_out=st3[:, sc:sc + 1]),
                 reads=[x1k], writes=[f"xn3{ti}", f"st3_{sc}"])
            S.op("act", lambda e: e.activation(out=st3[:, sc:sc + 1], in_=st3[:, sc:sc + 1], func=AF.Ln, scale=1.0 / DM, bias=EPS),
                 reads=[], writes=[f"st3_{sc}"])
            S.op("act", lambda e: e.activation(out=st3[:, sc:sc + 1], in_=st3[:, sc:sc + 1], func=AF.Exp, scale=-0.5),
                 reads=[], writes=[f"st3_{sc}"])
            S.op("dve", lambda e: e.scalar_tensor_tensor(out=xn3[ti][:], in0=x1[:, ti, :], scalar=st3[:, sc:sc + 1], in1=gb3[:, 1, :],
                                                         op0=ALU.mult, op1=ALU.mult),
                 reads=[x1k, f"st3_{sc}", "gb3"], writes=[f"xn3{ti}"])

        def front_tr(w, ti):
            pk = 6 + ti % 2
            pbf = ps[pk][:].bitcast(BF16)
            for kc in range(8):
                S.op("pe", lambda e, kc=kc: e.transpose(out=pbf[:, kc * 128:(kc + 1) * 128],
                                                        in_=xn3[ti][:, kc * 128:(kc + 1) * 128], identity=ident[:]),
                     reads=[f"xn3{ti}", "ident"], writes=[PK[pk]], signal=(kc == 7))
            S.op("act", lambda e: e.activation(out=h2T[:, :, 128 * ti:128 * ti + 128],
                                               in_=pbf.rearrange("p (k t) -> p k t", k=8), func=AF.Copy),
                 reads=[], writes=[PK[pk], "h2T"])

        def tail_tile(w, ti):
            x1 = x1s[w % 2]
            i = 4 * w + ti
            pk0 = 2 * (ti % 2)
            for cb in range(8):
                S.op("pe", lambda e, cb=cb: e.transpose(
                    out=ps[pk0 + cb // 4][:, 128 * (cb % 4):128 * (cb % 4) + 128], in_=fT[:, cb, 128 * ti:128 * ti + 128],
                    identity=identf[:]), reads=FT + ["identf"], writes=[PK[pk0 + cb // 4]], signal=(cb % 4 == 3))
            ob = xoctr[0] % 2
            xoctr[0] += 1
            post_norm(pk0, 2, x1[:, ti, :], f"x1_{w % 2}_{ti}", xo[ob][:], f"xo{ob}", 4 * (ti % 2) + 2)
            S.dma("sp", lambda e: e.dma_start(out=out_d[128 * i:128 * i + 128, :], in_=xo[ob][:]), f"xo{ob}",
                  reads=[f"xo{ob}"], writes=[f"outd{ob}"])

        front_mm(0, 0)
        front_mm(0, 1)
        front_tr(0, 0)
        front_mm(0, 2)
        front_tr(0, 1)
        front_mm(0, 3)
        front_tr(0, 2)
        front_tr(0, 3)
        HID = [f"hid{fb}" for fb in range(32)]
        FT = [f"fT{cb}" for cb in range(8)]
        for w in range(4):
            for fq in range(8):
                b = fq % 2
                S.dma("pool", lambda e, fq=fq, b=b: e.dma_start(out=w1b[b], in_=w1s_d[fq].rearrange("p (k c) -> p k c", k=8)),
                      f"w1b{b}", writes=[f"w1b{b}"])
                for f4 in range(4):
                    fb = 4 * fq + f4
                    pk = 4 + fb % 4
                    for kc in range(8):
                        mm(ps[pk][:, :], w1b[b][:, kc, 128 * f4:128 * f4 + 128], h2T[:, kc, :], kc == 0, kc == 7,
                           [f"w1b{b}", "h2T"], [PK[pk]], kc == 7)
                    S.op("act", lambda e, fb=fb, pk=pk: e.activation(out=hid[:, fb, :], in_=ps[pk][:, :], func=AF.Relu),
                         reads=[], writes=[PK[pk], f"hid{fb}"])
                    S.op("dve", lambda e, fb=fb: e.tensor_tensor(out=hid[:, fb, :], in0=hid[:, fb, :], in1=hid[:, fb, :], op=ALU.mult),
                         reads=[], writes=[f"hid{fb}"])
                if w > 0 and fq % 2 == 0:
                    tail_tile(w - 1, fq // 2)
                if w < 3 and fq % 2 == 1:
                    front_mm(w + 1, fq // 2)
            for cb in range(8):
                b = cb % 2
                S.dma("pool", lambda e, cb=cb, b=b: e.dma_start(
                    out=w2b[b][:], in_=w2s_d[cb].rearrange("p (f c) -> p f c", f=32)), f"w2b{b}", writes=[f"w2b{b}"])
                pk = 4 + cb % 2
                for fb in range(32):
                    mm(ps[pk][:, :], w2b[b][:, fb, :], hid[:, fb, :], fb == 0, fb == 31, [f"w2b{b}"] + (HID if fb == 0 else []),
                       [PK[pk]], fb == 31)
                S.op("act", lambda e, cb=cb, pk=pk: e.activation(out=fT[:, cb, :], in_=ps[pk][:, :], func=AF.Copy),
                     reads=[], writes=[PK[pk], f"fT{cb}"])
                if w < 3 and cb % 2 == 0:
                    front_tr(w + 1, cb // 2)
            if w == 3:
                for ti in range(4):
                    tail_tile(3, ti)
        S.barrier()
    late.close()
    top.close()
    S.emit()
    return nc


_CACHE = {}


def _consts():
    if "c" in _CACHE:
        return _CACHE["c"]
    bf = ml_dtypes.bfloat16
    ident = np.eye(128, dtype=np.float32)
    k = np.arange(128)[:, None].astype(np.float64)
    q = np.arange(128)[None, :].astype(np.float64)
    mask = (k <= q).astype(np.float32)
    eb = np.zeros((128, 80, 128), np.float32)
    dils = (1.0, 4.0, 16.0)
    for h in range(16):
        slope = 2.0 ** (-8.0 * (h + 1) / 16.0)
        for p in range(3):
            cur = np.where(k <= q, np.exp(-slope * dils[p] * (q - k)), 0.0)
            prev = np.where(k >= q, np.exp(-slope * dils[p] * (q + 128.0 - k)), 0.0)
            if p < 2:
                eb[:, h * 5 + 2 * p, :] = cur
                eb[:, h * 5 + 2 * p + 1, :] = prev
            else:
                eb[:, h * 5 + 4, :] = cur
    c = {"ident": ident.astype(bf), "identf": ident, "mask": mask, "onesf": np.ones((128, 128), np.float32),
         "eb": eb.astype(bf)}
    _CACHE["c"] = c
    return c


def kernel(x, norm_mix_pre, w_in, b_gate, ln_v_g, ln_v_b, w_s, b_s, w_a_proj, w_b_proj, w_out,
           norm_mix_post, norm_ffn_pre, w_ff1, w_ff2, norm_ffn_post):
    f = lambda a: np.ascontiguousarray(np.asarray(a, dtype=np.float32))
    x = f(x)

    def kmaj(w, k):
        w = f(w)
        return np.ascontiguousarray(w.reshape(k, 128, w.shape[-1]).transpose(1, 0, 2))

    shared = dict(_consts())
    def kblk(w):
        w = f(w)
        nb = w.shape[1] // 128
        return np.ascontiguousarray(w.reshape(8, 128, nb, 128).transpose(2, 1, 0, 3)).reshape(nb, 128, 1024)

    shared["w_in"] = kblk(np.asarray(w_in)[0])
    shared["w_a"] = kblk(np.asarray(w_a_proj)[0])
    shared["w_b"] = kblk(np.asarray(w_b_proj)[0])
    shared["w_o"] = kmaj(np.asarray(w_out)[0], 8)
    shared["w_1"] = kmaj(np.asarray(w_ff1)[0], 8)
    w2 = f(np.asarray(w_ff2)[0]).reshape(2, 16, 128, 8, 128).transpose(3, 0, 2, 1, 4)
    shared["w_2"] = np.ascontiguousarray(w2).reshape(16, 128, 2048)
    shared["wsT"] = np.ascontiguousarray(f(w_s)[0].transpose(2, 0, 1))
    shared["b_s"] = f(b_s)[0].reshape(1, 1024)
    shared["ln_g_row"] = f(ln_v_g)[0].reshape(1, 1024)
    shared["ln_b"] = np.ascontiguousarray(f(ln_v_b)[0].reshape(8, 128).T)
    shared["b_gate"] = np.ascontiguousarray(f(b_gate)[0].reshape(2, 8, 128).transpose(2, 0, 1).reshape(128, 16))
    shared["gains"] = np.ascontiguousarray(np.stack([f(norm_mix_pre)[0], f(norm_mix_post)[0], f(norm_ffn_pre)[0],
                                                     f(norm_ffn_post)[0]], axis=0))
    if "nc" not in _CACHE:
        _CACHE["nc"] = build_nc()
    nc = _CACHE["nc"]
    in_maps = []
    for c in range(NCORES):
        m = dict(shared)
        m["x"] = x[c]
        in_maps.append(m)
    res = run_bass_kernel_spmd(nc, in_maps, core_ids=list(range(NCORES)))
    return np.stack([np.asarray(r["out"], dtype=np.float32) for r in res.results], axis=0)
```

```python
import math
from contextlib import ExitStack
import numpy as np
import ml_dtypes
import concourse.bass as bass
import concourse.mybir as mybir
from concourse.bass_utils import run_bass_kernel_spmd

F32 = mybir.dt.float32
BF16 = mybir.dt.bfloat16
AF = mybir.ActivationFunctionType
ALU = mybir.AluOpType
EPS = 1e-6
SEQ = 2048
DM = 1024
NCORES = 8


class Sched:
    ENGS = ("pe", "act", "dve", "pool", "sp")

    def __init__(self, nc):
        self.nc = nc
        self.es = ExitStack()
        self.lists = {e: [] for e in self.ENGS}
        self.sem = {e: self.es.enter_context(nc.semaphore("s_" + e)) for e in self.ENGS}
        self.cnt = {e: 0 for e in self.ENGS}
        self.waited = {e: {} for e in self.ENGS}
        self.lastw = {}
        self.readers = {}
        self.dma_sems = {}
        self.dma_cnt = {}

    def _need(self, eng, dep):
        key, sem, val = dep
        if key == eng and eng == "pe":
            return
        w = self.waited[eng]
        if w.get(key, 0) >= val:
            return
        w[key] = val
        self.lists[eng].append(("wait", sem, val))

    def _deps(self, eng, reads, writes):
        for b in list(reads) + list(writes):
            d = self.lastw.get(b)
            if d is not None:
                self._need(eng, d)
        for b in writes:
            for d in self.readers.get(b, ()):
                self._need(eng, d)

    def _mark(self, tok, reads, writes):
        for b in writes:
            self.lastw[b] = tok
            self.readers[b] = []
        for b in reads:
            self.readers.setdefault(b, []).append(tok)

    def op(self, eng, fn, reads=(), writes=(), signal=True):
        self._deps(eng, reads, writes)
        if signal:
            self.cnt[eng] += 1
            self.lists[eng].append(("op", fn, self.sem[eng], 1))
            tok = (eng, self.sem[eng], self.cnt[eng])
        else:
            self.lists[eng].append(("op", fn, None, 0))
            tok = (eng, self.sem[eng], self.cnt[eng] + 1)
        self._mark(tok, reads, writes)

    def dma(self, queue, fn, chan, reads=(), writes=()):
        if chan not in self.dma_sems:
            self.dma_sems[chan] = self.es.enter_context(self.nc.semaphore("d_" + str(chan)))
            self.dma_cnt[chan] = 0
        self._deps(queue, reads, writes)
        self.dma_cnt[chan] += 16
        sem = self.dma_sems[chan]
        self.lists[queue].append(("op", fn, sem, 16))
        tok = ("dma_" + str(chan), sem, self.dma_cnt[chan])
        self._mark(tok, reads, writes)

    def barrier(self):
        toks = [(e, self.sem[e], self.cnt[e]) for e in ("pe", "act", "dve", "pool") if self.cnt[e] > 0]
        toks += [("dma_" + str(c), self.dma_sems[c], self.dma_cnt[c]) for c in self.dma_sems]
        for e in self.ENGS:
            for t in toks:
                if t[0] != e:
                    self._need(e, t)
                elif e != "pe":
                    self._need(e, t)

    def emit(self):
        nc = self.nc
        lists = self.lists

        def replay(lst, e):
            for it in lst:
                if it[0] == "wait":
                    e.wait_ge(it[1], it[2])
                else:
                    ins = it[1](e)
                    if it[2] is not None:
                        ins.then_inc(it[2], it[3])

        with nc.Block() as block:
            @block.tensor
            def _(e):
                replay(lists["pe"], e)

            @block.scalar
            def _(e):
                replay(lists["act"], e)

            @block.vector
            def _(e):
                replay(lists["dve"], e)

            @block.gpsimd
            def _(e):
                replay(lists["pool"], e)

            @block.sync
            def _(e):
                replay(lists["sp"], e)
        self.es.close()


def tokap(T, rows, p, blk):
    if p == 0:
        return T[rows, 128 * blk:128 * blk + 128]
    if p == 1:
        c, n = blk // 4, blk % 4
        s = 512 * n + c
        return T[rows, s:s + 509:4]
    return T[rows, blk:SEQ:16]


def build_nc():
    nc = bass.Bass("TRN2", target_bir_lowering=False)

    def din(name, shape, dt=F32):
        return nc.dram_tensor(name, list(shape), dt, kind="ExternalInput").ap()

    x_d = din("x", [SEQ, DM])
    win_d = din("w_in", [56, 128, 1024])
    wa_d = din("w_a", [8, 128, 1024])
    wb_d = din("w_b", [8, 128, 1024])
    wo_d = din("w_o", [128, 8, 1024])
    w1_d = din("w_1", [128, 8, 4096])
    w2_d = din("w_2", [16, 128, 2048])
    wsT_d = din("wsT", [128, 8, 128])
    bs_d = din("b_s", [1, 1024])
    lgrow_d = din("ln_g_row", [1, 1024])
    lb_d = din("ln_b", [128, 8])
    bg_d = din("b_gate", [128, 16])
    gains_d = din("gains", [4, 1024])
    ident_d = din("ident", [128, 128], BF16)
    identf_d = din("identf", [128, 128])
    mask_d = din("mask", [128, 128])
    ones_d = din("onesf", [128, 128])
    eb_d = din("eb", [128, 80, 128], BF16)
    out_d = nc.dram_tensor("out", [SEQ, DM], F32, kind="ExternalOutput").ap()
    den_d = nc.dram_tensor("den_scr", [16, SEQ], F32).ap()
    w1s_d = nc.dram_tensor("w1_bf", [8, 128, 4096], BF16).ap()
    w2s_d = nc.dram_tensor("w2_bf", [8, 128, 4096], BF16).ap()

    S = Sched(nc)
    top = ExitStack()

    def sb(stack, name, shape, dt):
        return stack.enter_context(nc.sbuf_tensor("sb_" + name, list(shape), dt))

    ps = [top.enter_context(nc.psum_tensor(f"pst{k}", [128, 512], F32)) for k in range(8)]
    PK = [f"ps{k}" for k in range(8)]
    ident = sb(top, "ident", [128, 128], BF16)
    identf = sb(top, "identf", [128, 128], F32)
    hT = sb(top, "hT", [128, 8, SEQ], BF16)
    yaT = sb(top, "yaT", [128, 8, SEQ], BF16)
    big1 = sb(top, "big1", [128, 16, 1024], BF16)
    gc = big1
    ybT = big1[:].rearrange("p a b -> p (a b)").rearrange("p (k t) -> p k t", k=8)
    stat = sb(top, "stat", [128, 64], F32)
    ssq = stat[:, 0:16]
    rstd = stat[:, 16:32]
    varall = stat[:, 32:48]
    rstdv = stat[:, 48:64]

    cvn = [0]

    pend = []

    def conv_step(stg):
        while pend:
            pend.pop(0)()
        if conv:
            conv.pop(0)(stg)

    def conv_w1(fq, h):
        def f(stg):
            b = cvn[0] % 2
            cvn[0] += 1
            S.dma("pool", lambda e: e.dma_start(out=stg[b][:].rearrange("p (k c) -> p k c", k=4),
                                                in_=w1_d[:, 4 * h:4 * h + 4, 512 * fq:512 * fq + 512]), f"stgl{b}", writes=[f"stg{b}"])
            pend.append(lambda: S.dma("sp", lambda e: e.dma_start(out=w1s_d[fq][:, 2048 * h:2048 * h + 2048], in_=stg[b][:]),
                                      f"stgs{b}", reads=[f"stg{b}"], writes=["w1s"]))
        return f

    def conv_w2(cb, h2):
        def f(stg):
            b = cvn[0] % 2
            cvn[0] += 1
            S.dma("pool", lambda e: e.dma_start(out=stg[b][:], in_=w2_d[2 * cb + h2]), f"stgl{b}", writes=[f"stg{b}"])
            pend.append(lambda: S.dma("sp", lambda e: e.dma_start(out=w2s_d[cb][:, 2048 * h2:2048 * h2 + 2048], in_=stg[b][:]),
                                      f"stgs{b}", reads=[f"stg{b}"], writes=["w2s"]))
        return f
    conv = [conv_w1(fq, h) for fq in range(8) for h in range(2)] + [conv_w2(cb, h2) for cb in range(8) for h2 in range(2)]
    S.dma("sp", lambda e: e.dma_start(out=ident[:], in_=ident_d[:, :]), "c0a", writes=["ident"])
    S.dma("sp", lambda e: e.dma_start(out=identf[:], in_=identf_d[:, :]), "c0b", writes=["identf"])

    def mm(out, lhsT, rhs, start, stop, reads, writes, signal):
        S.op("pe", lambda e: e.matmul(out, lhsT=lhsT, rhs=rhs, start=start, stop=stop, skip_group_check=True),
             reads=reads, writes=writes, signal=signal)

    def rms_rstd(col, n_in=1.0 / DM):
        S.op("act", lambda e: e.activation(out=col, in_=col, func=AF.Ln, scale=n_in, bias=EPS),
             reads=["stat"], writes=["stat"])
        S.op("act", lambda e: e.activation(out=col, in_=col, func=AF.Exp, scale=-0.5),
             reads=["stat"], writes=["stat"])

    with ExitStack() as ph:
        gbc = sb(ph, "gbc", [128, DM], F32)
        xt = [sb(ph, f"xt{i}", [128, DM], F32) for i in range(3)]
        xn = [sb(ph, f"xn{i}", [128, DM], BF16) for i in range(3)]
        junk = sb(ph, "junk", [128, DM], BF16)
        wv = sb(ph, "wv", [128, 8, 1024], BF16)
        gv = sb(ph, "gv", [128, DM], F32)
        bst = sb(ph, "bst", [128, 16], F32)
        wua = sb(ph, "wua", [128, 8, 1024], BF16)
        wsTf = sb(ph, "wsTf", [128, 8, 128], F32)
        maskt = sb(ph, "maskt", [128, 128], F32)
        onesf = sb(ph, "onesf", [128, 128], F32)
        BS = sb(ph, "BS", [128, 8, 128], F32)
        R = sb(ph, "R", [128, 8, 128], F32)
        lb = sb(ph, "lb", [128, 8], F32)
        wsn = [sb(ph, f"wsn{i}", [128, 8, 128], BF16) for i in range(2)]
        tmpm = [sb(ph, f"tmpm{i}", [128, 4, 128], BF16) for i in range(2)]
        lgbc = sb(ph, "lgbc", [128, DM], F32)
        S.dma("sp", lambda e: e.dma_start(out=lgbc[:], in_=lgrow_d[0:1, :].partition_broadcast(128)), "c1b", writes=["lgbc"])
        S.dma("sp", lambda e: e.dma_start(out=gbc[:], in_=gains_d[0:1, :].partition_broadcast(128)), "c1", writes=["gbc"])
        def wblk(d, blk):
            return d[blk].rearrange("p (k c) -> p k c", k=8)
        for j8 in range(8):
            S.dma("pool", lambda e, j8=j8: e.dma_start(out=wv[:, :, 128 * j8:128 * j8 + 128], in_=wblk(win_d, 8 + j8)),
                  "wv", writes=["wv"])
        def p1A(i):
            b = i % 3
            S.dma("sp", lambda e: e.dma_start(out=xt[b][:], in_=x_d[128 * i:128 * i + 128, :]), f"x{b}", writes=[f"xt{b}"])
            S.op("act", lambda e: e.activation(out=junk[:], in_=xt[b][:], func=AF.Square, accum_out=ssq[:, i:i + 1]),
                 reads=[f"xt{b}"], writes=["junk", f"ssq{i}"])
            S.op("act", lambda e: e.activation(out=rstd[:, i:i + 1], in_=ssq[:, i:i + 1], func=AF.Ln, scale=1.0 / DM, bias=EPS),
                 reads=[f"ssq{i}"], writes=[f"rstd{i}"])
            S.op("act", lambda e: e.activation(out=rstd[:, i:i + 1], in_=rstd[:, i:i + 1], func=AF.Exp, scale=-0.5),
                 reads=[], writes=[f"rstd{i}"])
            S.op("dve", lambda e: e.scalar_tensor_tensor(out=xn[b][:], in0=xt[b][:], scalar=rstd[:, i:i + 1],
                                                         in1=gbc[:], op0=ALU.mult, op1=ALU.mult),
                 reads=[f"xt{b}", f"rstd{i}", "gbc"], writes=[f"xn{b}"])

        def p1B(i):
            b = i % 3
            pk = i % 2
            pbf = ps[pk][:].bitcast(BF16)
            for kc in range(8):
                S.op("pe", lambda e, kc=kc: e.transpose(out=pbf[:, kc * 128:(kc + 1) * 128],
                                                        in_=xn[b][:, kc * 128:(kc + 1) * 128], identity=ident[:]),
                     reads=[f"xn{b}", "ident"], writes=[PK[pk]], signal=(kc == 7))
            S.op("dve", lambda e: e.tensor_copy(out=hT[:, :, 128 * i:128 * i + 128], in_=pbf.rearrange("p (k t) -> p k t", k=8)),
                 reads=[], writes=[PK[pk], f"hT{i}"])

        S.dma("sp", lambda e: e.dma_start(out=wsTf[:], in_=wsT_d[:, :, :]), "c2a", writes=["wsTf"])
        S.dma("sp", lambda e: e.dma_start(out=maskt[:], in_=mask_d[:, :]), "c2b", writes=["maskt"])
        S.dma("sp", lambda e: e.dma_start(out=onesf[:], in_=ones_d[:, :]), "c2c", writes=["onesf"])
        S.dma("sp", lambda e: e.dma_start(out=BS[:].rearrange("p g t -> p (g t)"), in_=bs_d[0:1, :].partition_broadcast(128)),
              "c2d", writes=["BS"])
        S.dma("sp", lambda e: e.dma_start(out=lb[:], in_=lb_d[:, :]), "c2f", writes=["lb"])
        for i in range(18):
            if i < 16:
                p1A(i)
            if i >= 2:
                p1B(i - 2)
        for j8 in range(8):
            S.dma("pool", lambda e, j8=j8: e.dma_start(out=wua[:, :, 128 * j8:128 * j8 + 128], in_=wblk(win_d, j8)),
                  "wua", reads=["hT9"], writes=["wua"])
        S.op("dve", lambda e: e.tensor_tensor(out=wsTf[:], in0=wsTf[:], in1=maskt[:].unsqueeze(1).broadcast_to([128, 8, 128]),
                                              op=ALU.mult), reads=["wsTf", "maskt"], writes=["wsTf"])
        for hf in range(2):
            mm(ps[4 + hf][:, :], onesf[:], wsTf[:, 4 * hf:4 * hf + 4, :].rearrange("p g t -> p (g t)"), True, True,
               ["onesf", "wsTf"], [PK[4 + hf]], True)
            for gi in range(4):
                g = 4 * hf + gi
                S.op("dve", lambda e, g=g, gi=gi, hf=hf: e.scalar_tensor_tensor(
                    out=R[:, g, :], in0=ps[4 + hf][:, 128 * gi:128 * gi + 128], scalar=lb[:, g:g + 1], in1=BS[:, g, :],
                    op0=ALU.mult, op1=ALU.add), reads=["lb", "BS"], writes=[PK[4 + hf], "R"])
        HT = [f"hT{i}" for i in range(16)]

        for n in range(16):
            pk = 2 * (n % 2)
            for hf in range(2):
                for kc in range(8):
                    mm(ps[pk + hf][:, :], hT[:, kc, 128 * n:128 * n + 128], wv[:, kc, 512 * hf:512 * hf + 512],
                       kc == 0, kc == 7, [f"hT{n}", "wv"], [PK[pk + hf]], kc == 7)
                S.op("act", lambda e, hf=hf, pk=pk: e.activation(out=gv[:, 512 * hf:512 * hf + 512], in_=ps[pk + hf][:, :],
                                                                 func=AF.Gelu_apprx_tanh),
                     reads=[], writes=[PK[pk + hf], f"gv{hf}"])
                S.op("dve", lambda e, hf=hf: e.bn_stats(out=bst[:, 6 * hf:6 * hf + 6], in_=gv[:, 512 * hf:512 * hf + 512]),
                     reads=[f"gv{hf}"], writes=["bst"])
            S.op("dve", lambda e: e.bn_aggr(out=bst[:, 12:14], in_=bst[:, 0:12]), reads=["bst"], writes=["bst"])
            S.op("dve", lambda e, n=n: e.scalar_tensor_tensor(out=gc[:, n, :], in0=gv[:], scalar=bst[:, 12:13], in1=lgbc[:],
                                                              op0=ALU.subtract, op1=ALU.mult),
                 reads=["gv0", "gv1", "bst", "lgbc"], writes=[f"gc{n}"])
            S.op("dve", lambda e, n=n: e.tensor_copy(out=varall[:, n:n + 1], in_=bst[:, 13:14]),
                 reads=["bst"], writes=["stat"])

        S.op("act", lambda e: e.activation(out=rstdv, in_=varall, func=AF.Ln, scale=1.0, bias=EPS),
             reads=["stat"], writes=["stat"])
        S.op("act", lambda e: e.activation(out=rstdv, in_=rstdv, func=AF.Exp, scale=-0.5),
             reads=["stat"], writes=["stat"])

        def mkwsn(n):
            b = n % 2
            S.op("dve", lambda e: e.tensor_scalar(out=wsn[b][:], in0=wsTf[:], scalar1=rstdv[:, n:n + 1], scalar2=1.0,
                                                  op0=ALU.mult, op1=ALU.mult),
                 reads=["wsTf", "stat"], writes=[f"wsn{b}"])

        def gate_mm(n):
            b = n % 2
            for hf in range(2):
                pk = (2 * n + hf) % 4
                for gi in range(4):
                    g = 4 * hf + gi
                    mm(ps[pk][:, 128 * gi:128 * gi + 128], gc[:, n, 128 * g:128 * g + 128], wsn[b][:, g, :], gi == 0, True,
                       [f"gc{n}", f"wsn{b}"], [PK[pk]], gi == 3)

        def gate_ev(n):
            for hf in range(2):
                pk = (2 * n + hf) % 4
                tb = (2 * n + hf) % 2
                S.op("dve", lambda e, hf=hf, pk=pk, tb=tb: e.tensor_tensor(
                    out=tmpm[tb][:].rearrange("p a t -> p (a t)"), in0=ps[pk][:, :],
                    in1=R[:, 4 * hf:4 * hf + 4, :].rearrange("p a t -> p (a t)"), op=ALU.add),
                    reads=["R"], writes=[PK[pk], f"tmpm{tb}"])
                yv = yaT[:, 4 * hf:4 * hf + 4, 128 * n:128 * n + 128]
                S.op("pool", lambda e, yv=yv, tb=tb: e.tensor_tensor(out=yv, in0=yv, in1=tmpm[tb][:], op=ALU.mult),
                     reads=[f"tmpm{tb}"], writes=[f"ya{c}_{n // 4}" for c in range(4 * hf, 4 * hf + 4)])

        gsteps = []
        def add_gate_window(w):
            for n in range(4 * w, 4 * w + 4):
                gsteps.append(lambda n=n: (mkwsn(n), gate_mm(n)))
                gsteps.append(lambda n=n: gate_ev(n))

        for w in range(4):
            for cb in range(8):
                pk = 4 + (w * 8 + cb) % 4
                for kc in range(8):
                    mm(ps[pk][:, :], wua[:, kc, 128 * cb:128 * cb + 128], hT[:, kc, 512 * w:512 * w + 512], kc == 0, kc == 7,
                       ["wua"] + HT[4 * w:4 * w + 4], [PK[pk]], kc == 7)
                S.op("act", lambda e, cb=cb, w=w, pk=pk: e.activation(out=yaT[:, cb, 512 * w:512 * w + 512], in_=ps[pk][:, :],
                                                                      func=AF.Gelu_apprx_tanh),
                     reads=[], writes=[PK[pk], f"ya{cb}_{w}"])
                if gsteps:
                    gsteps.pop(0)()
            add_gate_window(w)
        while gsteps:
            gsteps.pop(0)()
        S.barrier()

    YA = [f"ya{c}_{w}" for c in range(8) for w in range(4)]
    with ExitStack() as ph:
        EBh = [sb(ph, f"EBh{i}", [128, 10, 128], BF16) for i in range(2)]
        QZ = [[sb(ph, f"QZ{i}_{hh}", [128, SEQ], BF16) for hh in range(2)] for i in range(2)]
        KT = [sb(ph, f"KT{i}", [128, SEQ], BF16) for i in range(2)]
        VT = sb(ph, "VT", [128, SEQ], BF16)
        vaug = [sb(ph, f"vaug{i}", [128, 48, 2, 65], BF16) for i in range(2)]
        NET, NPT, LAG = 2, 7, 3
        stg = [sb(ph, f"stg{i}", [128, 2048], BF16) for i in range(2)]
        Et = [sb(ph, f"Et{i}", [128, 512], BF16) for i in range(NET)]
        PT = [sb(ph, f"PT{i}", [128, 512], BF16) for i in range(NPT)]
        wqkv = [sb(ph, f"wqkv{i}", [128, 8, 384], BF16) for i in range(2)]
        norm_base = nc.sbuf_base
        accs = [sb(ph, f"acc{i}", [128, SEQ], F32) for i in range(2)]
        denbc = sb(ph, "denbc", [64, 1024], F32)
        ytmp = sb(ph, "ytmp", [64, 1024], BF16)
        dsm = sb(ph, "dsm", [16, 128], F32)
        for i in range(2):
            S.op("dve", lambda e, i=i: e.memset(vaug[i][:].rearrange("p a b c -> p (a b c)"), 1.0), reads=[], writes=[f"vaug{i}"])
            S.op("dve", lambda e, i=i: e.memset(QZ[i][0][64:128, :], 0.0), reads=[], writes=[f"QZ{i}_0"])
            S.op("dve", lambda e, i=i: e.memset(QZ[i][1][0:64, :], 0.0), reads=[], writes=[f"QZ{i}_1"])
        SB_, AB_ = [2, 3, 4], [5, 6, 7]

        def tokapN(T, p, kb, nblk):
            if p == 0:
                return T[:, 128 * kb:128 * kb + 128 * nblk]
            if p == 1:
                c, n = kb // 4, kb % 4
                s = 512 * n + c
                return T[:, s:s + 4 * (128 * nblk - 1) + 1:4]
            return T[:, kb:SEQ:16]

        sctr = [0]
        actr = [0]
        pctr = [0]

        def prep_items(hp):
            d = hp % 2
            items = []

            def load():
                S.dma("sp", lambda e: e.dma_start(out=EBh[d][:], in_=eb_d[:, 10 * hp:10 * hp + 10, :]), f"ebh{d}", writes=[f"EBh{d}"])
                for j in range(3):
                    S.dma("pool", lambda e, j=j: e.dma_start(out=wqkv[d][:, :, 128 * j:128 * j + 128], in_=wblk(win_d, 16 + 8 * j + hp)),
                          f"wqkv{d}", writes=[f"wqkv{d}"])
            items.append(load)

            def proj(j, w):
                def f():
                    pk = pctr[0] % 2
                    pctr[0] += 1
                    for kc in range(8):
                        mm(ps[pk][:, :], wqkv[d][:, kc, 128 * j:128 * j + 128], hT[:, kc, 512 * w:512 * w + 512], kc == 0, kc == 7,
                           [f"wqkv{d}"] + HT[4 * w:4 * w + 4], [PK[pk]], kc == 7)
                    if j == 0:
                        for hh in range(2):
                            S.op("act", lambda e, hh=hh: e.activation(
                                out=QZ[d][hh][64 * hh:64 * hh + 64, 512 * w:512 * w + 512], in_=ps[pk][64 * hh:64 * hh + 64, :],
                                func=AF.Copy, scale=0.125), reads=[], writes=[PK[pk], f"QZ{d}_{hh}"])
                    elif j == 1:
                        S.op("act", lambda e: e.activation(out=KT[d][:, 512 * w:512 * w + 512], in_=ps[pk][:, :], func=AF.Copy),
                             reads=[], writes=[PK[pk], f"KT{d}"])
                    else:
                        S.op("act", lambda e: e.activation(out=VT[:, 512 * w:512 * w + 512], in_=ps[pk][:, :], func=AF.Copy),
                             reads=[], writes=[PK[pk], "VT"])
                return f
            for j in (2, 1, 0):
                for w in range(4):
                    items.append(proj(j, w))

            def vtr(p, half):
                def f():
                    pk = pctr[0] % 2
                    pctr[0] += 1
                    pbf = ps[pk][:].bitcast(BF16)
                    for bi in range(8):
                        blk = 8 * half + bi
                        S.op("pe", lambda e, blk=blk, bi=bi: e.transpose(
                            out=pbf[:, 128 * bi:128 * bi + 128], in_=tokap(VT, slice(0, 128), p, blk), identity=ident[:]),
                            reads=["VT", "ident"], writes=[PK[pk]], signal=(bi == 7))
                    b0 = 16 * p + 8 * half
                    S.op("act", lambda e: e.activation(
                        out=vaug[d][:, b0:b0 + 8, :, 0:64], in_=pbf.rearrange("p (a h d) -> p a h d", a=8, h=2), func=AF.Copy),
                        reads=[], writes=[PK[pk], f"vaug{d}"])
                return f
            its = items[:5] + items[5:9] + [vtr(p, half) for p in range(3) for half in range(2)] + items[9:]
            return its

        deferred = []
        gs = [0]

        def run_deferred(force=False):
            while deferred and (force or deferred[0][0] <= gs[0]):
                deferred.pop(0)[1]()

        for it in prep_items(0):
            it()
        for hp in range(8):
            d = hp % 2
            nxt = prep_items(hp + 1) if hp < 7 else []
            sjobs, pvjobs = [], []
            for hh in range(2):
                eb0 = 5 * hh
                base = len(sjobs)
                for j in range(8):
                    sjobs.append(dict(hh=hh, p=0, kbs=[(2 * j, 2), (2 * j + 1, 2 if 2 * j + 1 < 15 else 1)], ebi=eb0))
                for w in range(4):
                    parts = []
                    if w > 0:
                        kb = 4 * w - 1
                        parts.append((base + kb // 2, 384, 128, kb, 0))
                    for a in range(4):
                        kb = 4 * w + a
                        parts.append((base + kb // 2, 0 if kb % 2 == 0 else 256, 256 if a < 3 else 128, kb, 128 * a))
                    pvjobs.append(dict(hh=hh, p=0, grp=w, parts=parts, last=False))
                for c in range(4):
                    base2 = len(sjobs)
                    sjobs.append(dict(hh=hh, p=1, kbs=[(4 * c, 2), (4 * c + 1, 2)], ebi=eb0 + 2))
                    sjobs.append(dict(hh=hh, p=1, kbs=[(4 * c + 2, 2), (4 * c + 3, 1)], ebi=eb0 + 2))
                    parts = [(base2, 0, 256, 4 * c, 0), (base2, 256, 256, 4 * c + 1, 128),
                             (base2 + 1, 0, 256, 4 * c + 2, 256), (base2 + 1, 256, 128, 4 * c + 3, 384)]
                    pvjobs.append(dict(hh=hh, p=1, grp=c, parts=parts, last=False))
                for g in range(4):
                    base3 = len(sjobs)
                    sjobs.append(dict(hh=hh, p=2, kbs=[(4 * g + a, 1) for a in range(4)], ebi=eb0 + 4))
                    parts = [(base3, 128 * a, 128, 4 * g + a, 128 * a) for a in range(4)]
                    pvjobs.append(dict(hh=hh, p=2, grp=g, parts=parts, last=(g == 3)))

            def emit_s(si):
                sj = sjobs[si]
                p, hh, ebi = sj["p"], sj["hh"], sj["ebi"]
                EB = EBh[d]
                bank = SB_[sctr[0] % 3]
                sctr[0] += 1
                es, ptb = si % NET, si % NPT
                cw = 256 if p < 2 else 128
                nk = len(sj["kbs"])
                for i, (kb, nblk) in enumerate(sj["kbs"]):
                    mm(ps[bank][:, cw * i:cw * i + 128 * nblk], tokap(KT[d], slice(0, 128), p, kb), tokapN(QZ[d][hh], p, kb, nblk),
                       i == 0, True, [f"KT{d}", f"QZ{d}_{hh}"], [PK[bank]], i == nk - 1)
                W = cw * (nk - 1) + 128 * sj["kbs"][-1][1]
                S.op("act", lambda e: e.activation(out=Et[es][:, 0:W], in_=ps[bank][:, 0:W], func=AF.Exp),
                     reads=[], writes=[PK[bank], f"Et{es}"])
                rk = [f"Et{es}", f"EBh{d}"]
                if p == 2:
                    S.op("dve", lambda e: e.tensor_tensor(
                        out=PT[ptb][:, :].rearrange("p (a q) -> p a q", q=128), in0=Et[es][:, :].rearrange("p (a q) -> p a q", q=128),
                        in1=EB[:, ebi:ebi + 1, :].broadcast_to([128, 4, 128]), op=ALU.mult), reads=rk, writes=[f"PT{ptb}"])
                else:
                    eb2 = EB[:, ebi:ebi + 2, :].rearrange("p a q -> p (a q)")
                    if sj["kbs"][-1][1] == 2:
                        S.op("dve", lambda e: e.tensor_tensor(
                            out=PT[ptb][:, :].rearrange("p (a q) -> p a q", q=256), in0=Et[es][:, :].rearrange("p (a q) -> p a q", q=256),
                            in1=eb2.unsqueeze(1).broadcast_to([128, 2, 256]), op=ALU.mult), reads=rk, writes=[f"PT{ptb}"])
                    else:
                        S.op("dve", lambda e: e.tensor_tensor(out=PT[ptb][:, 0:256], in0=Et[es][:, 0:256], in1=eb2, op=ALU.mult),
                             reads=rk, writes=[f"PT{ptb}"])
                        S.op("dve", lambda e: e.tensor_tensor(out=PT[ptb][:, 256:384], in0=Et[es][:, 256:384], in1=EB[:, ebi, :],
                                                              op=ALU.mult), reads=rk, writes=[f"PT{ptb}"])

            def emit_pv(pj, hp=hp, d=d):
                p, hh, grp = pj["p"], pj["hh"], pj["grp"]
                h = 2 * hp + hh
                acc = accs[hh]
                ak = f"acc{hh}"
                bank = AB_[actr[0] % 3]
                actr[0] += 1
                np_ = len(pj["parts"])
                for i, (sidx, ptcol, n, vblk, acol) in enumerate(pj["parts"]):
                    mm(ps[bank][:65, acol:acol + n], vaug[d][:, 16 * p + vblk, hh, :], PT[sidx % NPT][:, ptcol:ptcol + n],
                       i == 0, True, [f"vaug{d}", f"PT{sidx % NPT}"], [PK[bank]], i == np_ - 1)
                if p == 0:
                    S.op("act", lambda e: e.activation(out=acc[0:65, 512 * grp:512 * grp + 512], in_=ps[bank][0:65, :], func=AF.Copy),
                         reads=[], writes=[PK[bank], ak])
                elif p == 1:
                    av = acc[0:65, :].rearrange("p (n i c) -> p c n i", n=4, c=4)[:, grp]
                    S.op("dve", lambda e: e.tensor_tensor(out=av, in0=ps[bank][0:65, :].rearrange("p (n i) -> p n i", n=4), in1=av,
                                                          op=ALU.add), reads=[], writes=[PK[bank], ak])
                else:
                    av = acc[0:65, :].rearrange("p (i r) -> p r i", r=16)[:, 4 * grp:4 * grp + 4, :]
                    S.op("dve", lambda e: e.tensor_tensor(out=av, in0=ps[bank][0:65, :].rearrange("p (r i) -> p r i", r=4), in1=av,
                                                          op=ALU.add), reads=[], writes=[PK[bank], ak])
                if pj["last"]:
                    def norm():
                        S.dma("sp", lambda e: e.dma_start(out=den_d[h:h + 1, :], in_=acc[64:65, :]), "den", reads=[ak], writes=["dend"])
                        S.dma("sp", lambda e: e.dma_start(out=dsm[:], in_=den_d[h].rearrange("(a b) -> a b", a=16)), "den",
                              reads=["dend"], writes=["dsm"])

                    def norm2():
                        S.op("act", lambda e: e.activation(out=dsm[:], in_=dsm[:], func=AF.Ln), reads=[], writes=["dsm"])
                        S.op("act", lambda e: e.activation(out=dsm[:], in_=dsm[:], func=AF.Exp, scale=-1.0), reads=[], writes=["dsm"])
                        S.dma("sp", lambda e: e.dma_start(out=den_d[h].rearrange("(a b) -> a b", a=16), in_=dsm[:]), "den",
                              reads=["dsm", "dend"], writes=["dend"])

                    def fin(hf):
                        def f():
                            c0 = 1024 * hf
                            S.dma("sp", lambda e: e.dma_start(out=denbc[:], in_=den_d[h:h + 1, c0:c0 + 1024].partition_broadcast(64)),
                                  "den", reads=["dend", "denbc"], writes=["denbc"])
                            if hh == 0:
                                S.op("dve", lambda e: e.tensor_tensor(out=ybT[0:64, hp, c0:c0 + 1024], in0=acc[0:64, c0:c0 + 1024],
                                                                      in1=denbc[:], op=ALU.mult),
                                     reads=[ak, "denbc"], writes=[f"yb{hp}"])
                            else:
                                S.op("dve", lambda e: e.tensor_tensor(out=ytmp[:], in0=acc[0:64, c0:c0 + 1024], in1=denbc[:], op=ALU.mult),
                                     reads=[ak, "denbc"], writes=["ytmp"])
                                S.dma("sp", lambda e: e.dma_start(out=ybT[64:128, hp, c0:c0 + 1024], in_=ytmp[:]), "ymv",
                                      reads=["ytmp"], writes=[f"yb{hp}"])
                        return f
                    deferred.append((gs[0] + 1, norm))
                    deferred.append((gs[0] + 5, norm2))
                    deferred.append((gs[0] + 10, fin(0)))
                    deferred.append((gs[0] + 13, fin(1)))

            ptr = 0
            ni = 0
            for si in range(len(sjobs)):
                emit_s(si)
                gs[0] += 1
                while ptr < len(pvjobs) and max(x[0] for x in pvjobs[ptr]["parts"]) <= si - LAG:
                    emit_pv(pvjobs[ptr])
                    ptr += 1
                run_deferred()
                if si % 2 == 1 and ni < len(nxt):
                    nxt[ni]()
                    ni += 1
                if si % 8 == 4 and (conv or pend):
                    conv_step(stg)
            while ptr < len(pvjobs):
                emit_pv(pvjobs[ptr])
                ptr += 1
            while ni < len(nxt):
                nxt[ni]()
                ni += 1
        while conv or pend:
            conv_step(stg)
        S.barrier()
        run_deferred(force=True)

    YB = [f"yb{hp}" for hp in range(8)]
    late = ExitStack()
    merged = sb(late, "merged", [128, 8, SEQ], BF16)
    wout = sb(late, "wout", [128, 8, 1024], BF16)
    gb3 = sb(late, "gb3", [128, 3, DM], F32)
    for k in range(3):
        S.dma("sp", lambda e, k=k: e.dma_start(out=gb3[:, k, :], in_=gains_d[k + 1:k + 2, :].partition_broadcast(128)),
              "c5", writes=["gb3"])
    with ExitStack() as ph:
        wg = [sb(ph, f"wg{i}", [128, 8, 512], BF16) for i in range(2)]
        bg = sb(ph, "bg", [128, 16], F32)
        sg = [sb(ph, f"sg{i}", [128, 512], F32) for i in range(2)]
        tA = sb(ph, "tA", [128, 512], F32)
        assert nc.sbuf_base <= norm_base, (nc.sbuf_base, norm_base)
        S.dma("sp", lambda e: e.dma_start(out=bg[:], in_=bg_d[:, :]), "c4", writes=["bg"])
        S.op("dve", lambda e: e.tensor_scalar(out=bg[:], in0=bg[:], scalar1=-1.0, scalar2=0.0, op0=ALU.mult, op1=ALU.add),
             reads=["bg"], writes=["bg"])
        for cb in range(8):
            b = cb % 2
            srcs = [wblk(win_d, 40 + cb), wblk(win_d, 48 + cb), wblk(wa_d, cb), wblk(wb_d, cb)]
            for j in range(4):
                S.dma("pool", lambda e, j=j, b=b, src=srcs[j]: e.dma_start(out=wg[b][:, :, 128 * j:128 * j + 128], in_=src),
                      f"wg{b}", writes=[f"wg{b}"])
            if cb == 1:
                for hf in range(2):
                    S.dma("pool", lambda e, hf=hf: e.dma_start(out=wout[:, :, 512 * hf:512 * hf + 512],
                                                               in_=wo_d[:, :, 512 * hf:512 * hf + 512]), "wout", writes=["wout"])
            for w in range(4):
                acts = [hT, hT, yaT, ybT]
                rd = [HT[4 * w:4 * w + 4], HT[4 * w:4 * w + 4], YA, YB]
                po = 4 * ((cb * 4 + w) % 2)
                for j in range(4):
                    pk = po + j
                    for kc in range(8):
                        mm(ps[pk][:, :], wg[b][:, kc, 128 * j:128 * j + 128], acts[j][:, kc, 512 * w:512 * w + 512],
                           kc == 0, kc == 7, [f"wg{b}"] + rd[j], [PK[pk]], kc == 7)
                for j in range(2):
                    S.op("act", lambda e, j=j, cb=cb, po=po: e.activation(out=sg[j][:], in_=ps[po + j][:, :], func=AF.Exp, scale=-1.0,
                                                                   bias=bg[:, 8 * j + cb:8 * j + cb + 1]),
                         reads=["bg"], writes=[PK[po + j], f"sg{j}"])
                    S.op("act", lambda e, j=j: e.activation(out=sg[j][:], in_=sg[j][:], func=AF.Ln, scale=1.0, bias=1.0),
                         reads=[f"sg{j}"], writes=[f"sg{j}"])
                    S.op("act", lambda e, j=j: e.activation(out=sg[j][:], in_=sg[j][:], func=AF.Exp, scale=-1.0),
                         reads=[f"sg{j}"], writes=[f"sg{j}"])
                S.op("dve", lambda e, po=po: e.tensor_tensor(out=tA[:], in0=ps[po + 2][:, :], in1=sg[0][:], op=ALU.mult),
                     reads=["sg0"], writes=[PK[po + 2], "tA"])
                S.op("dve", lambda e, po=po: e.tensor_tensor(out=sg[1][:], in0=ps[po + 3][:, :], in1=sg[1][:], op=ALU.mult),
                     reads=["sg1"], writes=[PK[po + 3], "sg1"])
                S.op("dve", lambda e, cb=cb, w=w: e.tensor_tensor(out=merged[:, cb, 512 * w:512 * w + 512], in0=tA[:], in1=sg[1][:],
                                                                  op=ALU.add),
                     reads=["tA", "sg1"], writes=[f"mg{w}"])
        S.barrier()

    hid = hT[:].rearrange("p k t -> p (k t)").rearrange("p (f t) -> p f t", f=32)
    yaf = yaT[:].rearrange("p k t -> p (k t)").bitcast(F32)
    fT = yaf[:, 4096:8192].rearrange("p (c t) -> p c t", c=8)
    b1f = big1[:].rearrange("p a b -> p (a b)")
    w1b = [b1f[:, 8192 + 4096 * i:8192 + 4096 * (i + 1)].rearrange("p (k c) -> p k c", k=8) for i in range(2)]
    with ExitStack() as ph:
        xo = [sb(ph, f"xo{i}", [128, DM], F32) for i in range(2)]
        x1b = b1f[:, 0:8192].bitcast(F32).rearrange("p (t d) -> p t d", t=4)
        x1s = [yaf[:, 0:4096].rearrange("p (t d) -> p t d", t=4), x1b]
        t1 = sb(ph, "t1", [128, DM], F32)
        xn3 = [sb(ph, f"xn3{i}", [128, DM], BF16) for i in range(4)]
        junk3 = sb(ph, "junk3", [128, 512], BF16)
        h2T = sb(ph, "h2T", [128, 8, 512], BF16)
        w2b = [sb(ph, f"w2b{i}", [128, 32, 128], BF16) for i in range(2)]
        st3 = sb(ph, "st3", [128, 16], F32)
        xoctr = [0]

        def post_norm(pk0, gidx, addin, addkey, outap, outkey, scol):
            sk = f"st3_{scol}"
            for hf in range(2):
                S.op("act", lambda e, hf=hf: e.activation(out=junk3[:], in_=ps[pk0 + hf][:, :], func=AF.Square,
                                                          accum_out=st3[:, scol + hf:scol + hf + 1]),
                     reads=[], writes=[PK[pk0 + hf], "junk3", sk])
            S.op("dve", lambda e: e.tensor_tensor(out=st3[:, scol:scol + 1], in0=st3[:, scol:scol + 1], in1=st3[:, scol + 1:scol + 2],
                                                  op=ALU.add), reads=[], writes=[sk])
            S.op("act", lambda e: e.activation(out=st3[:, scol:scol + 1], in_=st3[:, scol:scol + 1], func=AF.Ln, scale=1.0 / DM, bias=EPS),
                 reads=[], writes=[sk])
            S.op("act", lambda e: e.activation(out=st3[:, scol:scol + 1], in_=st3[:, scol:scol + 1], func=AF.Exp, scale=-0.5),
                 reads=[], writes=[sk])
            for hf in range(2):
                S.op("dve", lambda e, hf=hf: e.scalar_tensor_tensor(
                    out=t1[:, 512 * hf:512 * hf + 512], in0=ps[pk0 + hf][:, :], scalar=st3[:, scol:scol + 1],
                    in1=gb3[:, gidx, 512 * hf:512 * hf + 512], op0=ALU.mult, op1=ALU.mult),
                    reads=[sk, "gb3"], writes=[PK[pk0 + hf], "t1"])
            S.op("dve", lambda e: e.tensor_tensor(out=outap, in0=t1[:], in1=addin, op=ALU.add),
                 reads=["t1", addkey], writes=[outkey])

        def front_mm(w, ti):
            i = 4 * w + ti
            x1 = x1s[w % 2]
            b = xoctr[0] % 2
            xoctr[0] += 1
            S.dma("sp", lambda e: e.dma_start(out=xo[b][:], in_=x_d[128 * i:128 * i + 128, :]), f"xo{b}", writes=[f"xo{b}"])
            pk0 = 2 * (ti % 2)
            for hf in range(2):
                for kc in range(8):
                    mm(ps[pk0 + hf][:, :], merged[:, kc, 128 * i:128 * i + 128], wout[:, kc, 512 * hf:512 * hf + 512],
                       kc == 0, kc == 7, [f"mg{w}", "wout"], [PK[pk0 + hf]], kc == 7)
            x1k = f"x1_{w % 2}_{ti}"
            post_norm(pk0, 0, xo[b][:], f"xo{b}", x1[:, ti, :], x1k, 4 * (ti % 2))
            sc = 8 + ti
            S.op("act", lambda e: e.activation(out=xn3[ti][:], in_=x1[:, ti, :], func=AF.Square, accum_out=st3[:, sc:sc + 1]),
                 reads=[x1k], writes=[f"xn3{ti}", f"st3_{sc}"])
            S.op("act", lambda e: e.activation(out=st3[:, sc:sc + 1], in_=st3[:, sc:sc + 1], func=AF.Ln, scale=1.0 / DM, bias=EPS),
                 reads=[], writes=[f"st3_{sc}"])
            S.op("act", lambda e: e.activation(out=st3[:, sc:sc + 1], in_=st3[:, sc:sc + 1], func=AF.Exp, scale=-0.5),
                 reads=[], writes=[f"st3_{sc}"])
            S.op("dve", lambda e: e.scalar_tensor_tensor(out=xn3[ti][:], in0=x1[:, ti, :], scalar=st3[:, sc:sc + 1], in1=gb3[:, 1, :],
                                                         op0=ALU.mult, op1=ALU.mult),
                 reads=[x1k, f"st3_{sc}", "gb3"], writes=[f"xn3{ti}"])

        def front_tr(w, ti):
            pk = 6 + ti % 2
            pbf = ps[pk][:].bitcast(BF16)
            for kc in range(8):
                S.op("pe", lambda e, kc=kc: e.transpose(out=pbf[:, kc * 128:(kc + 1) * 128],
                                                        in_=xn3[ti][:, kc * 128:(kc + 1) * 128], identity=ident[:]),
                     reads=[f"xn3{ti}", "ident"], writes=[PK[pk]], signal=(kc == 7))
            S.op("act", lambda e: e.activation(out=h2T[:, :, 128 * ti:128 * ti + 128],
                                               in_=pbf.rearrange("p (k t) -> p k t", k=8), func=AF.Copy),
                 reads=[], writes=[PK[pk], "h2T"])

        def tail_tile(w, ti):
            x1 = x1s[w % 2]
            i = 4 * w + ti
            pk0 = 2 * (ti % 2)
            for cb in range(8):
                S.op("pe", lambda e, cb=cb: e.transpose(
                    out=ps[pk0 + cb // 4][:, 128 * (cb % 4):128 * (cb % 4) + 128], in_=fT[:, cb, 128 * ti:128 * ti + 128],
                    identity=identf[:]), reads=FT + ["identf"], writes=[PK[pk0 + cb // 4]], signal=(cb % 4 == 3))
            ob = xoctr[0] % 2
            xoctr[0] += 1
            post_norm(pk0, 2, x1[:, ti, :], f"x1_{w % 2}_{ti}", xo[ob][:], f"xo{ob}", 4 * (ti % 2) + 2)
            S.dma("sp", lambda e: e.dma_start(out=out_d[128 * i:128 * i + 128, :], in_=xo[ob][:]), f"xo{ob}",
                  reads=[f"xo{ob}"], writes=[f"outd{ob}"])

        front_mm(0, 0)
        front_mm(0, 1)
        front_tr(0, 0)
        front_mm(0, 2)
        front_tr(0, 1)
        front_mm(0, 3)
        front_tr(0, 2)
        front_tr(0, 3)
        HID = [f"hid{fb}" for fb in range(32)]
        FT = [f"fT{cb}" for cb in range(8)]
        for w in range(4):
            for fq in range(8):
                b = fq % 2
                S.dma("pool", lambda e, fq=fq, b=b: e.dma_start(out=w1b[b], in_=w1s_d[fq].rearrange("p (k c) -> p k c", k=8)),
                      f"w1b{b}", writes=[f"w1b{b}"])
                for f4 in range(4):
                    fb = 4 * fq + f4
                    pk = 4 + fb % 4
                    for kc in range(8):
                        mm(ps[pk][:, :], w1b[b][:, kc, 128 * f4:128 * f4 + 128], h2T[:, kc, :], kc == 0, kc == 7,
                           [f"w1b{b}", "h2T"], [PK[pk]], kc == 7)
                    S.op("act", lambda e, fb=fb, pk=pk: e.activation(out=hid[:, fb, :], in_=ps[pk][:, :], func=AF.Relu),
                         reads=[], writes=[PK[pk], f"hid{fb}"])
                    S.op("dve", lambda e, fb=fb: e.tensor_tensor(out=hid[:, fb, :], in0=hid[:, fb, :], in1=hid[:, fb, :], op=ALU.mult),
                         reads=[], writes=[f"hid{fb}"])
                if w > 0 and fq % 2 == 0:
                    tail_tile(w - 1, fq // 2)
                if w < 3 and fq % 2 == 1:
                    front_mm(w + 1, fq // 2)
            for cb in range(8):
                b = cb % 2
                S.dma("pool", lambda e, cb=cb, b=b: e.dma_start(
                    out=w2b[b][:], in_=w2s_d[cb].rearrange("p (f c) -> p f c", f=32)), f"w2b{b}", writes=[f"w2b{b}"])
                pk = 4 + cb % 2
                for fb in range(32):
                    mm(ps[pk][:, :], w2b[b][:, fb, :], hid[:, fb, :], fb == 0, fb == 31, [f"w2b{b}"] + (HID if fb == 0 else []),
                       [PK[pk]], fb == 31)
                S.op("act", lambda e, cb=cb, pk=pk: e.activation(out=fT[:, cb, :], in_=ps[pk][:, :], func=AF.Copy),
                     reads=[], writes=[PK[pk], f"fT{cb}"])
                if w < 3 and cb % 2 == 0:
                    front_tr(w + 1, cb // 2)
            if w == 3:
                for ti in range(4):
                    tail_tile(3, ti)
        S.barrier()
    late.close()
    top.close()
    S.emit()
    return nc


_CACHE = {}


def _consts():
    if "c" in _CACHE:
        return _CACHE["c"]
    bf = ml_dtypes.bfloat16
    ident = np.eye(128, dtype=np.float32)
    k = np.arange(128)[:, None].astype(np.float64)
    q = np.arange(128)[None, :].astype(np.float64)
    mask = (k <= q).astype(np.float32)
    eb = np.zeros((128, 80, 128), np.float32)
    dils = (1.0, 4.0, 16.0)
    for h in range(16):
        slope = 2.0 ** (-8.0 * (h + 1) / 16.0)
        for p in range(3):
            cur = np.where(k <= q, np.exp(-slope * dils[p] * (q - k)), 0.0)
            prev = np.where(k >= q, np.exp(-slope * dils[p] * (q + 128.0 - k)), 0.0)
            if p < 2:
                eb[:, h * 5 + 2 * p, :] = cur
                eb[:, h * 5 + 2 * p + 1, :] = prev
            else:
                eb[:, h * 5 + 4, :] = cur
    c = {"ident": ident.astype(bf), "identf": ident, "mask": mask, "onesf": np.ones((128, 128), np.float32),
         "eb": eb.astype(bf)}
    _CACHE["c"] = c
    return c


def kernel(x, norm_mix_pre, w_in, b_gate, ln_v_g, ln_v_b, w_s, b_s, w_a_proj, w_b_proj, w_out,
           norm_mix_post, norm_ffn_pre, w_ff1, w_ff2, norm_ffn_post):
    f = lambda a: np.ascontiguousarray(np.asarray(a, dtype=np.float32))
    x = f(x)

    def kmaj(w, k):
        w = f(w)
        return np.ascontiguousarray(w.reshape(k, 128, w.shape[-1]).transpose(1, 0, 2))

    shared = dict(_consts())
    def kblk(w):
        w = f(w)
        nb = w.shape[1] // 128
        return np.ascontiguousarray(w.reshape(8, 128, nb, 128).transpose(2, 1, 0, 3)).reshape(nb, 128, 1024)

    shared["w_in"] = kblk(np.asarray(w_in)[0])
    shared["w_a"] = kblk(np.asarray(w_a_proj)[0])
    shared["w_b"] = kblk(np.asarray(w_b_proj)[0])
    shared["w_o"] = kmaj(np.asarray(w_out)[0], 8)
    shared["w_1"] = kmaj(np.asarray(w_ff1)[0], 8)
    w2 = f(np.asarray(w_ff2)[0]).reshape(2, 16, 128, 8, 128).transpose(3, 0, 2, 1, 4)
    shared["w_2"] = np.ascontiguousarray(w2).reshape(16, 128, 2048)
    shared["wsT"] = np.ascontiguousarray(f(w_s)[0].transpose(2, 0, 1))
    shared["b_s"] = f(b_s)[0].reshape(1, 1024)
    shared["ln_g_row"] = f(ln_v_g)[0].reshape(1, 1024)
    shared["ln_b"] = np.ascontiguousarray(f(ln_v_b)[0].reshape(8, 128).T)
    shared["b_gate"] = np.ascontiguousarray(f(b_gate)[0].reshape(2, 8, 128).transpose(2, 0, 1).reshape(128, 16))
    shared["gains"] = np.ascontiguousarray(np.stack([f(norm_mix_pre)[0], f(norm_mix_post)[0], f(norm_ffn_pre)[0],
                                                     f(norm_ffn_post)[0]], axis=0))
    if "nc" not in _CACHE:
        _CACHE["nc"] = build_nc()
    nc = _CACHE["nc"]
    in_maps = []
    for c in range(NCORES):
        m = dict(shared)
        m["x"] = x[c]
        in_maps.append(m)
    res = run_bass_kernel_spmd(nc, in_maps, core_ids=list(range(NCORES)))
    return np.stack([np.asarray(r["out"], dtype=np.float32) for r in res.results], axis=0)
```

```python
import math
from contextlib import ExitStack
import numpy as np
import ml_dtypes
import concourse.bass as bass
import concourse.mybir as mybir
from concourse.bass_utils import run_bass_kernel_spmd

F32 = mybir.dt.float32
BF16 = mybir.dt.bfloat16
AF = mybir.ActivationFunctionType
ALU = mybir.AluOpType
EPS = 1e-6
SEQ = 2048
DM = 1024
NCORES = 8


class Sched:
    ENGS = ("pe", "act", "dve", "pool", "sp")

    def __init__(self, nc):
        self.nc = nc
        self.es = ExitStack()
        self.lists = {e: [] for e in self.ENGS}
        self.sem = {e: self.es.enter_context(nc.semaphore("s_" + e)) for e in self.ENGS}
        self.cnt = {e: 0 for e in self.ENGS}
        self.waited = {e: {} for e in self.ENGS}
        self.lastw = {}
        self.readers = {}
        self.dma_sems = {}
        self.dma_cnt = {}

    def _need(self, eng, dep):
        key, sem, val = dep
        if key == eng and eng == "pe":
            return
        w = self.waited[eng]
        if w.get(key, 0) >= val:
            return
        w[key] = val
        self.lists[eng].append(("wait", sem, val))

    def _deps(self, eng, reads, writes):
        for b in list(reads) + list(writes):
            d = self.lastw.get(b)
            if d is not None:
                self._need(eng, d)
        for b in writes:
            for d in self.readers.get(b, ()):
                self._need(eng, d)

    def _mark(self, tok, reads, writes):
        for b in writes:
            self.lastw[b] = tok
            self.readers[b] = []
        for b in reads:
            self.readers.setdefault(b, []).append(tok)

    def op(self, eng, fn, reads=(), writes=(), signal=True):
        self._deps(eng, reads, writes)
        if signal:
            self.cnt[eng] += 1
            self.lists[eng].append(("op", fn, self.sem[eng], 1))
            tok = (eng, self.sem[eng], self.cnt[eng])
        else:
            self.lists[eng].append(("op", fn, None, 0))
            tok = (eng, self.sem[eng], self.cnt[eng] + 1)
        self._mark(tok, reads, writes)

    def dma(self, queue, fn, chan, reads=(), writes=()):
        if chan not in self.dma_sems:
            self.dma_sems[chan] = self.es.enter_context(self.nc.semaphore("d_" + str(chan)))
            self.dma_cnt[chan] = 0
        self._deps(queue, reads, writes)
        self.dma_cnt[chan] += 16
        sem = self.dma_sems[chan]
        self.lists[queue].append(("op", fn, sem, 16))
        tok = ("dma_" + str(chan), sem, self.dma_cnt[chan])
        self._mark(tok, reads, writes)

    def barrier(self):
        toks = [(e, self.sem[e], self.cnt[e]) for e in ("pe", "act", "dve", "pool") if self.cnt[e] > 0]
        toks += [("dma_" + str(c), self.dma_sems[c], self.dma_cnt[c]) for c in self.dma_sems]
        for e in self.ENGS:
            for t in toks:
                if t[0] != e:
                    self._need(e, t)
                elif e != "pe":
                    self._need(e, t)

    def emit(self):
        nc = self.nc
        lists = self.lists

        def replay(lst, e):
            for it in lst:
                if it[0] == "wait":
                    e.wait_ge(it[1], it[2])
                else:
                    ins = it[1](e)
                    if it[2] is not None:
                        ins.then_inc(it[2], it[3])

        with nc.Block() as block:
            @block.tensor
            def _(e):
                replay(lists["pe"], e)

            @block.scalar
            def _(e):
                replay(lists["act"], e)

            @block.vector
            def _(e):
                replay(lists["dve"], e)

            @block.gpsimd
            def _(e):
                replay(lists["pool"], e)

            @block.sync
            def _(e):
                replay(lists["sp"], e)
        self.es.close()


def tokap(T, rows, p, blk):
    if p == 0:
        return T[rows, 128 * blk:128 * blk + 128]
    if p == 1:
        c, n = blk // 4, blk % 4
        s = 512 * n + c
        return T[rows, s:s + 509:4]
    return T[rows, blk:SEQ:16]


def build_nc():
    nc = bass.Bass("TRN2", target_bir_lowering=False)

    def din(name, shape, dt=F32):
        return nc.dram_tensor(name, list(shape), dt, kind="ExternalInput").ap()

    x_d = din("x", [SEQ, DM])
    win_d = din("w_in", [56, 128, 1024])
    wa_d = din("w_a", [8, 128, 1024])
    wb_d = din("w_b", [8, 128, 1024])
    wo_d = din("w_o", [128, 8, 1024])
    w1_d = din("w_1", [128, 8, 4096])
    w2_d = din("w_2", [16, 128, 2048])
    wsT_d = din("wsT", [128, 8, 128])
    bs_d = din("b_s", [1, 1024])
    lgrow_d = din("ln_g_row", [1, 1024])
    lb_d = din("ln_b", [128, 8])
    bg_d = din("b_gate", [128, 16])
    gains_d = din("gains", [4, 1024])
    ident_d = din("ident", [128, 128], BF16)
    identf_d = din("identf", [128, 128])
    mask_d = din("mask", [128, 128])
    ones_d = din("onesf", [128, 128])
    eb_d = din("eb", [128, 80, 128], BF16)
    out_d = nc.dram_tensor("out", [SEQ, DM], F32, kind="ExternalOutput").ap()
    den_d = nc.dram_tensor("den_scr", [16, SEQ], F32).ap()
    w1s_d = nc.dram_tensor("w1_bf", [8, 128, 4096], BF16).ap()
    w2s_d = nc.dram_tensor("w2_bf", [8, 128, 4096], BF16).ap()

    S = Sched(nc)
    top = ExitStack()

    def sb(stack, name, shape, dt):
        return stack.enter_context(nc.sbuf_tensor("sb_" + name, list(shape), dt))

    ps = [top.enter_context(nc.psum_tensor(f"pst{k}", [128, 512], F32)) for k in range(8)]
    PK = [f"ps{k}" for k in range(8)]
    ident = sb(top, "ident", [128, 128], BF16)
    identf = sb(top, "identf", [128, 128], F32)
    hT = sb(top, "hT", [128, 8, SEQ], BF16)
    yaT = sb(top, "yaT", [128, 8, SEQ], BF16)
    big1 = sb(top, "big1", [128, 16, 1024], BF16)
    gc = big1
    ybT = big1[:].rearrange("p a b -> p (a b)").rearrange("p (k t) -> p k t", k=8)
    stat = sb(top, "stat", [128, 64], F32)
    ssq = stat[:, 0:16]
    rstd = stat[:, 16:32]
    varall = stat[:, 32:48]
    rstdv = stat[:, 48:64]

    cvn = [0]

    pend = []

    def conv_step(stg):
        while pend:
            pend.pop(0)()
        if conv:
            conv.pop(0)(stg)

    def conv_w1(fq, h):
        def f(stg):
            b = cvn[0] % 2
            cvn[0] += 1
            S.dma("pool", lambda e: e.dma_start(out=stg[b][:].rearrange("p (k c) -> p k c", k=4),
                                                in_=w1_d[:, 4 * h:4 * h + 4, 512 * fq:512 * fq + 512]), f"stgl{b}", writes=[f"stg{b}"])
            pend.append(lambda: S.dma("sp", lambda e: e.dma_start(out=w1s_d[fq][:, 2048 * h:2048 * h + 2048], in_=stg[b][:]),
                                      f"stgs{b}", reads=[f"stg{b}"], writes=["w1s"]))
        return f

    def conv_w2(cb, h2):
        def f(stg):
            b = cvn[0] % 2
            cvn[0] += 1
            S.dma("pool", lambda e: e.dma_start(out=stg[b][:], in_=w2_d[2 * cb + h2]), f"stgl{b}", writes=[f"stg{b}"])
            pend.append(lambda: S.dma("sp", lambda e: e.dma_start(out=w2s_d[cb][:, 2048 * h2:2048 * h2 + 2048], in_=stg[b][:]),
                                      f"stgs{b}", reads=[f"stg{b}"], writes=["w2s"]))
        return f
    conv = [conv_w1(fq, h) for fq in range(8) for h in range(2)] + [conv_w2(cb, h2) for cb in range(8) for h2 in range(2)]
    S.dma("sp", lambda e: e.dma_start(out=ident[:], in_=ident_d[:, :]), "c0a", writes=["ident"])
    S.dma("sp", lambda e: e.dma_start(out=identf[:], in_=identf_d[:, :]), "c0b", writes=["identf"])

    def mm(out, lhsT, rhs, start, stop, reads, writes, signal):
        S.op("pe", lambda e: e.matmul(out, lhsT=lhsT, rhs=rhs, start=start, stop=stop, skip_group_check=True),
             reads=reads, writes=writes, signal=signal)

    def rms_rstd(col, n_in=1.0 / DM):
        S.op("act", lambda e: e.activation(out=col, in_=col, func=AF.Ln, scale=n_in, bias=EPS),
             reads=["stat"], writes=["stat"])
        S.op("act", lambda e: e.activation(out=col, in_=col, func=AF.Exp, scale=-0.5),
             reads=["stat"], writes=["stat"])

    with ExitStack() as ph:
        gbc = sb(ph, "gbc", [128, DM], F32)
        xt = [sb(ph, f"xt{i}", [128, DM], F32) for i in range(3)]
        xn = [sb(ph, f"xn{i}", [128, DM], BF16) for i in range(3)]
        junk = sb(ph, "junk", [128, DM], BF16)
        wv = sb(ph, "wv", [128, 8, 1024], BF16)
        gv = sb(ph, "gv", [128, DM], F32)
        bst = sb(ph, "bst", [128, 16], F32)
        wua = sb(ph, "wua", [128, 8, 1024], BF16)
        wsTf = sb(ph, "wsTf", [128, 8, 128], F32)
        maskt = sb(ph, "maskt", [128, 128], F32)
        onesf = sb(ph, "onesf", [128, 128], F32)
        BS = sb(ph, "BS", [128, 8, 128], F32)
        R = sb(ph, "R", [128, 8, 128], F32)
        lb = sb(ph, "lb", [128, 8], F32)
        wsn = [sb(ph, f"wsn{i}", [128, 8, 128], BF16) for i in range(2)]
        tmpm = [sb(ph, f"tmpm{i}", [128, 4, 128], BF16) for i in range(2)]
        lgbc = sb(ph, "lgbc", [128, DM], F32)
        S.dma("sp", lambda e: e.dma_start(out=lgbc[:], in_=lgrow_d[0:1, :].partition_broadcast(128)), "c1b", writes=["lgbc"])
        S.dma("sp", lambda e: e.dma_start(out=gbc[:], in_=gains_d[0:1, :].partition_broadcast(128)), "c1", writes=["gbc"])
        def wblk(d, blk):
            return d[blk].rearrange("p (k c) -> p k c", k=8)
        for j8 in range(8):
            S.dma("pool", lambda e, j8=j8: e.dma_start(out=wv[:, :, 128 * j8:128 * j8 + 128], in_=wblk(win_d, 8 + j8)),
                  "wv", writes=["wv"])
        def p1A(i):
            b = i % 3
            S.dma("sp", lambda e: e.dma_start(out=xt[b][:], in_=x_d[128 * i:128 * i + 128, :]), f"x{b}", writes=[f"xt{b}"])
            S.op("act", lambda e: e.activation(out=junk[:], in_=xt[b][:], func=AF.Square, accum_out=ssq[:, i:i + 1]),
                 reads=[f"xt{b}"], writes=["junk", f"ssq{i}"])
            S.op("act", lambda e: e.activation(out=rstd[:, i:i + 1], in_=ssq[:, i:i + 1], func=AF.Ln, scale=1.0 / DM, bias=EPS),
                 reads=[f"ssq{i}"], writes=[f"rstd{i}"])
            S.op("act", lambda e: e.activation(out=rstd[:, i:i + 1], in_=rstd[:, i:i + 1], func=AF.Exp, scale=-0.5),
                 reads=[], writes=[f"rstd{i}"])
            S.op("dve", lambda e: e.scalar_tensor_tensor(out=xn[b][:], in0=xt[b][:], scalar=rstd[:, i:i + 1],
                                                         in1=gbc[:], op0=ALU.mult, op1=ALU.mult),
                 reads=[f"xt{b}", f"rstd{i}", "gbc"], writes=[f"xn{b}"])

        def p1B(i):
            b = i % 3
            pk = i % 2
            pbf = ps[pk][:].bitcast(BF16)
            for kc in range(8):
                S.op("pe", lambda e, kc=kc: e.transpose(out=pbf[:, kc * 128:(kc + 1) * 128],
                                                        in_=xn[b][:, kc * 128:(kc + 1) * 128], identity=ident[:]),
                     reads=[f"xn{b}", "ident"], writes=[PK[pk]], signal=(kc == 7))
            S.op("dve", lambda e: e.tensor_copy(out=hT[:, :, 128 * i:128 * i + 128], in_=pbf.rearrange("p (k t) -> p k t", k=8)),
                 reads=[], writes=[PK[pk], f"hT{i}"])

        S.dma("sp", lambda e: e.dma_start(out=wsTf[:], in_=wsT_d[:, :, :]), "c2a", writes=["wsTf"])
        S.dma("sp", lambda e: e.dma_start(out=maskt[:], in_=mask_d[:, :]), "c2b", writes=["maskt"])
        S.dma("sp", lambda e: e.dma_start(out=onesf[:], in_=ones_d[:, :]), "c2c", writes=["onesf"])
        S.dma("sp", lambda e: e.dma_start(out=BS[:].rearrange("p g t -> p (g t)"), in_=bs_d[0:1, :].partition_broadcast(128)),
              "c2d", writes=["BS"])
        S.dma("sp", lambda e: e.dma_start(out=lb[:], in_=lb_d[:, :]), "c2f", writes=["lb"])
        for i in range(18):
            if i < 16:
                p1A(i)
            if i >= 2:
                p1B(i - 2)
        for j8 in range(8):
            S.dma("pool", lambda e, j8=j8: e.dma_start(out=wua[:, :, 128 * j8:128 * j8 + 128], in_=wblk(win_d, j8)),
                  "wua", reads=["hT9"], writes=["wua"])
        S.op("dve", lambda e: e.tensor_tensor(out=wsTf[:], in0=wsTf[:], in1=maskt[:].unsqueeze(1).broadcast_to([128, 8, 128]),
                                              op=ALU.mult), reads=["wsTf", "maskt"], writes=["wsTf"])
        for hf in range(2):
            mm(ps[4 + hf][:, :], onesf[:], wsTf[:, 4 * hf:4 * hf + 4, :].rearrange("p g t -> p (g t)"), True, True,
               ["onesf", "wsTf"], [PK[4 + hf]], True)
            for gi in range(4):
                g = 4 * hf + gi
                S.op("dve", lambda e, g=g, gi=gi, hf=hf: e.scalar_tensor_tensor(
                    out=R[:, g, :], in0=ps[4 + hf][:, 128 * gi:128 * gi + 128], scalar=lb[:, g:g + 1], in1=BS[:, g, :],
                    op0=ALU.mult, op1=ALU.add), reads=["lb", "BS"], writes=[PK[4 + hf], "R"])
        HT = [f"hT{i}" for i in range(16)]

        for n in range(16):
            pk = 2 * (n % 2)
            for hf in range(2):
                for kc in range(8):
                    mm(ps[pk + hf][:, :], hT[:, kc, 128 * n:128 * n + 128], wv[:, kc, 512 * hf:512 * hf + 512],
                       kc == 0, kc == 7, [f"hT{n}", "wv"], [PK[pk + hf]], kc == 7)
                S.op("act", lambda e, hf=hf, pk=pk: e.activation(out=gv[:, 512 * hf:512 * hf + 512], in_=ps[pk + hf][:, :],
                                                                 func=AF.Gelu_apprx_tanh),
                     reads=[], writes=[PK[pk + hf], f"gv{hf}"])
                S.op("dve", lambda e, hf=hf: e.bn_stats(out=bst[:, 6 * hf:6 * hf + 6], in_=gv[:, 512 * hf:512 * hf + 512]),
                     reads=[f"gv{hf}"], writes=["bst"])
            S.op("dve", lambda e: e.bn_aggr(out=bst[:, 12:14], in_=bst[:, 0:12]), reads=["bst"], writes=["bst"])
            S.op("dve", lambda e, n=n: e.scalar_tensor_tensor(out=gc[:, n, :], in0=gv[:], scalar=bst[:, 12:13], in1=lgbc[:],
                                                              op0=ALU.subtract, op1=ALU.mult),
                 reads=["gv0", "gv1", "bst", "lgbc"], writes=[f"gc{n}"])
            S.op("dve", lambda e, n=n: e.tensor_copy(out=varall[:, n:n + 1], in_=bst[:, 13:14]),
                 reads=["bst"], writes=["stat"])

        def emit_rstdv():
            S.op("act", lambda e: e.activation(out=rstdv, in_=varall, func=AF.Ln, scale=1.0, bias=EPS),
                 reads=["stat"], writes=["stat"])
            S.op("act", lambda e: e.activation(out=rstdv, in_=rstdv, func=AF.Exp, scale=-0.5),
                 reads=["stat"], writes=["stat"])

        def mkwsn(n):
            b = n % 2
            S.op("dve", lambda e: e.tensor_scalar(out=wsn[b][:], in0=wsTf[:], scalar1=rstdv[:, n:n + 1], scalar2=1.0,
                                                  op0=ALU.mult, op1=ALU.mult),
                 reads=["wsTf", "stat"], writes=[f"wsn{b}"])

        def gate_mm(n):
            b = n % 2
            for hf in range(2):
                pk = (2 * n + hf) % 4
                for gi in range(4):
                    g = 4 * hf + gi
                    mm(ps[pk][:, 128 * gi:128 * gi + 128], gc[:, n, 128 * g:128 * g + 128], wsn[b][:, g, :], gi == 0, True,
                       [f"gc{n}", f"wsn{b}"], [PK[pk]], gi == 3)

        def gate_ev(n):
            for hf in range(2):
                pk = (2 * n + hf) % 4
                tb = (2 * n + hf) % 2
                S.op("dve", lambda e, hf=hf, pk=pk, tb=tb: e.tensor_tensor(
                    out=tmpm[tb][:].rearrange("p a t -> p (a t)"), in0=ps[pk][:, :],
                    in1=R[:, 4 * hf:4 * hf + 4, :].rearrange("p a t -> p (a t)"), op=ALU.add),
                    reads=["R"], writes=[PK[pk], f"tmpm{tb}"])
                yv = yaT[:, 4 * hf:4 * hf + 4, 128 * n:128 * n + 128]
                S.op("pool", lambda e, yv=yv, tb=tb: e.tensor_tensor(out=yv, in0=yv, in1=tmpm[tb][:], op=ALU.mult),
                     reads=[f"tmpm{tb}"], writes=[f"ya{c}_{n // 4}" for c in range(4 * hf, 4 * hf + 4)])

        gsteps = []
        def add_gate_window(w):
            for n in range(4 * w, 4 * w + 4):
                gsteps.append(lambda n=n: (mkwsn(n), gate_mm(n)))
                gsteps.append(lambda n=n: gate_ev(n))

        for w in range(4):
            for cb in range(8):
                pk = 4 + (w * 8 + cb) % 4
                for kc in range(8):
                    mm(ps[pk][:, :], wua[:, kc, 128 * cb:128 * cb + 128], hT[:, kc, 512 * w:512 * w + 512], kc == 0, kc == 7,
                       ["wua"] + HT[4 * w:4 * w + 4], [PK[pk]], kc == 7)
                S.op("act", lambda e, cb=cb, w=w, pk=pk: e.activation(out=yaT[:, cb, 512 * w:512 * w + 512], in_=ps[pk][:, :],
                                                                      func=AF.Gelu_apprx_tanh),
                     reads=[], writes=[PK[pk], f"ya{cb}_{w}"])
                if gsteps:
                    gsteps.pop(0)()
            if w == 0:
                emit_rstdv()
            add_gate_window(w)
        while gsteps:
            gsteps.pop(0)()
        S.barrier()

    YA = [f"ya{c}_{w}" for c in range(8) for w in range(4)]
    with ExitStack() as ph:
        EBh = [sb(ph, f"EBh{i}", [128, 10, 128], BF16) for i in range(2)]
        QZ = [[sb(ph, f"QZ{i}_{hh}", [128, SEQ], BF16) for hh in range(2)] for i in range(2)]
        KT = [sb(ph, f"KT{i}", [128, SEQ], BF16) for i in range(2)]
        VT = sb(ph, "VT", [128, SEQ], BF16)
        vaug = [sb(ph, f"vaug{i}", [128, 48, 2, 65], BF16) for i in range(2)]
        NET, NPT, LAG = 2, 7, 3
        stg = [sb(ph, f"stg{i}", [128, 2048], BF16) for i in range(2)]
        Et = [sb(ph, f"Et{i}", [128, 512], BF16) for i in range(NET)]
        PT = [sb(ph, f"PT{i}", [128, 512], BF16) for i in range(NPT)]
        wqkv = [sb(ph, f"wqkv{i}", [128, 8, 384], BF16) for i in range(2)]
        norm_base = nc.sbuf_base
        accs = [sb(ph, f"acc{i}", [128, SEQ], F32) for i in range(2)]
        denbc = sb(ph, "denbc", [64, 1024], F32)
        ytmp = sb(ph, "ytmp", [64, 1024], BF16)
        dsm = sb(ph, "dsm", [16, 128], F32)
        for i in range(2):
            S.op("dve", lambda e, i=i: e.memset(vaug[i][:].rearrange("p a b c -> p (a b c)"), 1.0), reads=[], writes=[f"vaug{i}"])
            S.op("dve", lambda e, i=i: e.memset(QZ[i][0][64:128, :], 0.0), reads=[], writes=[f"QZ{i}_0"])
            S.op("dve", lambda e, i=i: e.memset(QZ[i][1][0:64, :], 0.0), reads=[], writes=[f"QZ{i}_1"])
        SB_, AB_ = [2, 3, 4], [5, 6, 7]

        def tokapN(T, p, kb, nblk):
            if p == 0:
                return T[:, 128 * kb:128 * kb + 128 * nblk]
            if p == 1:
                c, n = kb // 4, kb % 4
                s = 512 * n + c
                return T[:, s:s + 4 * (128 * nblk - 1) + 1:4]
            return T[:, kb:SEQ:16]

        sctr = [0]
        actr = [0]
        pctr = [0]

        def prep_items(hp):
            d = hp % 2
            items = []

            def load():
                S.dma("sp", lambda e: e.dma_start(out=EBh[d][:], in_=eb_d[:, 10 * hp:10 * hp + 10, :]), f"ebh{d}", writes=[f"EBh{d}"])
                for j in range(3):
                    S.dma("pool", lambda e, j=j: e.dma_start(out=wqkv[d][:, :, 128 * j:128 * j + 128], in_=wblk(win_d, 16 + 8 * j + hp)),
                          f"wqkv{d}", writes=[f"wqkv{d}"])
            items.append(load)

            def proj(j, w):
                def f():
                    pk = pctr[0] % 2
                    pctr[0] += 1
                    for kc in range(8):
                        mm(ps[pk][:, :], wqkv[d][:, kc, 128 * j:128 * j + 128], hT[:, kc, 512 * w:512 * w + 512], kc == 0, kc == 7,
                           [f"wqkv{d}"] + HT[4 * w:4 * w + 4], [PK[pk]], kc == 7)
                    if j == 0:
                        for hh in range(2):
                            S.op("act", lambda e, hh=hh: e.activation(
                                out=QZ[d][hh][64 * hh:64 * hh + 64, 512 * w:512 * w + 512], in_=ps[pk][64 * hh:64 * hh + 64, :],
                                func=AF.Copy, scale=0.125), reads=[], writes=[PK[pk], f"QZ{d}_{hh}"])
                    elif j == 1:
                        S.op("act", lambda e: e.activation(out=KT[d][:, 512 * w:512 * w + 512], in_=ps[pk][:, :], func=AF.Copy),
                             reads=[], writes=[PK[pk], f"KT{d}"])
                    else:
                        S.op("act", lambda e: e.activation(out=VT[:, 512 * w:512 * w + 512], in_=ps[pk][:, :], func=AF.Copy),
                             reads=[], writes=[PK[pk], "VT"])
                return f
            for j in (2, 1, 0):
                for w in range(4):
                    items.append(proj(j, w))

            def vtr(p, half):
                def f():
                    pk = pctr[0] % 2
                    pctr[0] += 1
                    pbf = ps[pk][:].bitcast(BF16)
                    for bi in range(8):
                        blk = 8 * half + bi
                        S.op("pe", lambda e, blk=blk, bi=bi: e.transpose(
                            out=pbf[:, 128 * bi:128 * bi + 128], in_=tokap(VT, slice(0, 128), p, blk), identity=ident[:]),
                            reads=["VT", "ident"], writes=[PK[pk]], signal=(bi == 7))
                    b0 = 16 * p + 8 * half
                    S.op("act", lambda e: e.activation(
                        out=vaug[d][:, b0:b0 + 8, :, 0:64], in_=pbf.rearrange("p (a h d) -> p a h d", a=8, h=2), func=AF.Copy),
                        reads=[], writes=[PK[pk], f"vaug{d}"])
                return f
            its = items[:5] + items[5:9] + [vtr(p, half) for p in range(3) for half in range(2)] + items[9:]
            return its

        deferred = []
        gs = [0]

        def run_deferred(force=False):
            while deferred and (force or deferred[0][0] <= gs[0]):
                deferred.pop(0)[1]()

        for it in prep_items(0):
            it()
        for hp in range(8):
            d = hp % 2
            nxt = prep_items(hp + 1) if hp < 7 else []
            sjobs, pvjobs = [], []
            for hh in range(2):
                eb0 = 5 * hh
                base = len(sjobs)
                for j in range(8):
                    sjobs.append(dict(hh=hh, p=0, kbs=[(2 * j, 2), (2 * j + 1, 2 if 2 * j + 1 < 15 else 1)], ebi=eb0))
                for w in range(4):
                    parts = []
                    if w > 0:
                        kb = 4 * w - 1
                        parts.append((base + kb // 2, 384, 128, kb, 0))
                    for a in range(4):
                        kb = 4 * w + a
                        parts.append((base + kb // 2, 0 if kb % 2 == 0 else 256, 256 if a < 3 else 128, kb, 128 * a))
                    pvjobs.append(dict(hh=hh, p=0, grp=w, parts=parts, last=False))
                for c in range(4):
                    base2 = len(sjobs)
                    sjobs.append(dict(hh=hh, p=1, kbs=[(4 * c, 2), (4 * c + 1, 2)], ebi=eb0 + 2))
                    sjobs.append(dict(hh=hh, p=1, kbs=[(4 * c + 2, 2), (4 * c + 3, 1)], ebi=eb0 + 2))
                    parts = [(base2, 0, 256, 4 * c, 0), (base2, 256, 256, 4 * c + 1, 128),
                             (base2 + 1, 0, 256, 4 * c + 2, 256), (base2 + 1, 256, 128, 4 * c + 3, 384)]
                    pvjobs.append(dict(hh=hh, p=1, grp=c, parts=parts, last=False))
                for g in range(4):
                    base3 = len(sjobs)
                    sjobs.append(dict(hh=hh, p=2, kbs=[(4 * g + a, 1) for a in range(4)], ebi=eb0 + 4))
                    parts = [(base3, 128 * a, 128, 4 * g + a, 128 * a) for a in range(4)]
                    pvjobs.append(dict(hh=hh, p=2, grp=g, parts=parts, last=(g == 3)))

            def emit_s(si):
                sj = sjobs[si]
                p, hh, ebi = sj["p"], sj["hh"], sj["ebi"]
                EB = EBh[d]
                bank = SB_[sctr[0] % 3]
                sctr[0] += 1
                es, ptb = si % NET, si % NPT
                cw = 256 if p < 2 else 128
                nk = len(sj["kbs"])
                for i, (kb, nblk) in enumerate(sj["kbs"]):
                    mm(ps[bank][:, cw * i:cw * i + 128 * nblk], tokap(KT[d], slice(0, 128), p, kb), tokapN(QZ[d][hh], p, kb, nblk),
                       i == 0, True, [f"KT{d}", f"QZ{d}_{hh}"], [PK[bank]], i == nk - 1)
                W = cw * (nk - 1) + 128 * sj["kbs"][-1][1]
                S.op("act", lambda e: e.activation(out=Et[es][:, 0:W], in_=ps[bank][:, 0:W], func=AF.Exp),
                     reads=[], writes=[PK[bank], f"Et{es}"])
                rk = [f"Et{es}", f"EBh{d}"]
                if p == 2:
                    S.op("dve", lambda e: e.tensor_tensor(
                        out=PT[ptb][:, :].rearrange("p (a q) -> p a q", q=128), in0=Et[es][:, :].rearrange("p (a q) -> p a q", q=128),
                        in1=EB[:, ebi:ebi + 1, :].broadcast_to([128, 4, 128]), op=ALU.mult), reads=rk, writes=[f"PT{ptb}"])
                else:
                    eb2 = EB[:, ebi:ebi + 2, :].rearrange("p a q -> p (a q)")
                    if sj["kbs"][-1][1] == 2:
                        S.op("dve", lambda e: e.tensor_tensor(
                            out=PT[ptb][:, :].rearrange("p (a q) -> p a q", q=256), in0=Et[es][:, :].rearrange("p (a q) -> p a q", q=256),
                            in1=eb2.unsqueeze(1).broadcast_to([128, 2, 256]), op=ALU.mult), reads=rk, writes=[f"PT{ptb}"])
                    else:
                        S.op("dve", lambda e: e.tensor_tensor(out=PT[ptb][:, 0:256], in0=Et[es][:, 0:256], in1=eb2, op=ALU.mult),
                             reads=rk, writes=[f"PT{ptb}"])
                        S.op("dve", lambda e: e.tensor_tensor(out=PT[ptb][:, 256:384], in0=Et[es][:, 256:384], in1=EB[:, ebi, :],
                                                              op=ALU.mult), reads=rk, writes=[f"PT{ptb}"])

            def emit_pv(pj, hp=hp, d=d):
                p, hh, grp = pj["p"], pj["hh"], pj["grp"]
                h = 2 * hp + hh
                acc = accs[hh]
                ak = f"acc{hh}"
                bank = AB_[actr[0] % 3]
                actr[0] += 1
                np_ = len(pj["parts"])
                for i, (sidx, ptcol, n, vblk, acol) in enumerate(pj["parts"]):
                    mm(ps[bank][:65, acol:acol + n], vaug[d][:, 16 * p + vblk, hh, :], PT[sidx % NPT][:, ptcol:ptcol + n],
                       i == 0, True, [f"vaug{d}", f"PT{sidx % NPT}"], [PK[bank]], i == np_ - 1)
                if p == 0:
                    S.op("act", lambda e: e.activation(out=acc[0:65, 512 * grp:512 * grp + 512], in_=ps[bank][0:65, :], func=AF.Copy),
                         reads=[], writes=[PK[bank], ak])
                elif p == 1:
                    av = acc[0:65, :].rearrange("p (n i c) -> p c n i", n=4, c=4)[:, grp]
                    S.op("dve", lambda e: e.tensor_tensor(out=av, in0=ps[bank][0:65, :].rearrange("p (n i) -> p n i", n=4), in1=av,
                                                          op=ALU.add), reads=[], writes=[PK[bank], ak])
                else:
                    av = acc[0:65, :].rearrange("p (i r) -> p r i", r=16)[:, 4 * grp:4 * grp + 4, :]
                    S.op("dve", lambda e: e.tensor_tensor(out=av, in0=ps[bank][0:65, :].rearrange("p (r i) -> p r i", r=4), in1=av,
                                                          op=ALU.add), reads=[], writes=[PK[bank], ak])
                if pj["last"]:
                    def norm():
                        S.dma("sp", lambda e: e.dma_start(out=den_d[h:h + 1, :], in_=acc[64:65, :]), "den", reads=[ak], writes=["dend"])
                        S.dma("sp", lambda e: e.dma_start(out=dsm[:], in_=den_d[h].rearrange("(a b) -> a b", a=16)), "den",
                              reads=["dend"], writes=["dsm"])

                    def norm2():
                        S.op("act", lambda e: e.activation(out=dsm[:], in_=dsm[:], func=AF.Ln), reads=[], writes=["dsm"])
                        S.op("act", lambda e: e.activation(out=dsm[:], in_=dsm[:], func=AF.Exp, scale=-1.0), reads=[], writes=["dsm"])
                        S.dma("sp", lambda e: e.dma_start(out=den_d[h].rearrange("(a b) -> a b", a=16), in_=dsm[:]), "den",
                              reads=["dsm", "dend"], writes=["dend"])

                    def fin(hf):
                        def f():
                            c0 = 1024 * hf
                            S.dma("sp", lambda e: e.dma_start(out=denbc[:], in_=den_d[h:h + 1, c0:c0 + 1024].partition_broadcast(64)),
                                  "den", reads=["dend", "denbc"], writes=["denbc"])
                            if hh == 0:
                                S.op("dve", lambda e: e.tensor_tensor(out=ybT[0:64, hp, c0:c0 + 1024], in0=acc[0:64, c0:c0 + 1024],
                                                                      in1=denbc[:], op=ALU.mult),
                                     reads=[ak, "denbc"], writes=[f"yb{hp}"])
                            else:
                                S.op("dve", lambda e: e.tensor_tensor(out=ytmp[:], in0=acc[0:64, c0:c0 + 1024], in1=denbc[:], op=ALU.mult),
                                     reads=[ak, "denbc"], writes=["ytmp"])
                                S.dma("sp", lambda e: e.dma_start(out=ybT[64:128, hp, c0:c0 + 1024], in_=ytmp[:]), "ymv",
                                      reads=["ytmp"], writes=[f"yb{hp}"])
                        return f
                    deferred.append((gs[0] + 1, norm))
                    deferred.append((gs[0] + 5, norm2))
                    deferred.append((gs[0] + 10, fin(0)))
                    deferred.append((gs[0] + 13, fin(1)))

            ptr = 0
            ni = 0
            for si in range(len(sjobs)):
                emit_s(si)
                gs[0] += 1
                while ptr < len(pvjobs) and max(x[0] for x in pvjobs[ptr]["parts"]) <= si - LAG:
                    emit_pv(pvjobs[ptr])
                    ptr += 1
                run_deferred()
                if si % 2 == 1 and ni < len(nxt):
                    nxt[ni]()
                    ni += 1
                if si % 10 == 5 and (conv or pend):
                    conv_step(stg)
            while ptr < len(pvjobs):
                emit_pv(pvjobs[ptr])
                ptr += 1
            while ni < len(nxt):
                nxt[ni]()
                ni += 1
        while conv or pend:
            conv_step(stg)
        S.barrier()
        run_deferred(force=True)

    YB = [f"yb{hp}" for hp in range(8)]
    late = ExitStack()
    merged = sb(late, "merged", [128, 8, SEQ], BF16)
    wout = sb(late, "wout", [128, 8, 1024], BF16)
    gb3 = sb(late, "gb3", [128, 3, DM], F32)
    for k in range(3):
        S.dma("sp", lambda e, k=k: e.dma_start(out=gb3[:, k, :], in_=gains_d[k + 1:k + 2, :].partition_broadcast(128)),
              "c5", writes=["gb3"])
    with ExitStack() as ph:
        wg = [sb(ph, f"wg{i}", [128, 8, 512], BF16) for i in range(2)]
        bg = sb(ph, "bg", [128, 16], F32)
        sg = [sb(ph, f"sg{i}", [128, 512], F32) for i in range(2)]
        tA = sb(ph, "tA", [128, 512], F32)
        assert nc.sbuf_base <= norm_base, (nc.sbuf_base, norm_base)
        S.dma("sp", lambda e: e.dma_start(out=bg[:], in_=bg_d[:, :]), "c4", writes=["bg"])
        S.op("dve", lambda e: e.tensor_scalar(out=bg[:], in0=bg[:], scalar1=-1.0, scalar2=0.0, op0=ALU.mult, op1=ALU.add),
             reads=["bg"], writes=["bg"])
        for cb in range(8):
            b = cb % 2
            srcs = [wblk(win_d, 40 + cb), wblk(win_d, 48 + cb), wblk(wa_d, cb), wblk(wb_d, cb)]
            for j in range(4):
                S.dma("pool", lambda e, j=j, b=b, src=srcs[j]: e.dma_start(out=wg[b][:, :, 128 * j:128 * j + 128], in_=src),
                      f"wg{b}", writes=[f"wg{b}"])
            if cb == 1:
                for hf in range(2):
                    S.dma("pool", lambda e, hf=hf: e.dma_start(out=wout[:, :, 512 * hf:512 * hf + 512],
                                                               in_=wo_d[:, :, 512 * hf:512 * hf + 512]), "wout", writes=["wout"])
            for w in range(4):
                acts = [hT, hT, yaT, ybT]
                rd = [HT[4 * w:4 * w + 4], HT[4 * w:4 * w + 4], YA, YB]
                po = 4 * ((cb * 4 + w) % 2)
                for j in range(4):
                    pk = po + j
                    for kc in range(8):
                        mm(ps[pk][:, :], wg[b][:, kc, 128 * j:128 * j + 128], acts[j][:, kc, 512 * w:512 * w + 512],
                           kc == 0, kc == 7, [f"wg{b}"] + rd[j], [PK[pk]], kc == 7)
                for j in range(2):
                    S.op("act", lambda e, j=j, cb=cb, po=po: e.activation(out=sg[j][:], in_=ps[po + j][:, :], func=AF.Exp, scale=-1.0,
                                                                   bias=bg[:, 8 * j + cb:8 * j + cb + 1]),
                         reads=["bg"], writes=[PK[po + j], f"sg{j}"])
                    S.op("act", lambda e, j=j: e.activation(out=sg[j][:], in_=sg[j][:], func=AF.Ln, scale=1.0, bias=1.0),
                         reads=[f"sg{j}"], writes=[f"sg{j}"])
                    S.op("act", lambda e, j=j: e.activation(out=sg[j][:], in_=sg[j][:], func=AF.Exp, scale=-1.0),
                         reads=[f"sg{j}"], writes=[f"sg{j}"])
                S.op("dve", lambda e, po=po: e.tensor_tensor(out=tA[:], in0=ps[po + 2][:, :], in1=sg[0][:], op=ALU.mult),
                     reads=["sg0"], writes=[PK[po + 2], "tA"])
                S.op("dve", lambda e, po=po: e.tensor_tensor(out=sg[1][:], in0=ps[po + 3][:, :], in1=sg[1][:], op=ALU.mult),
                     reads=["sg1"], writes=[PK[po + 3], "sg1"])
                S.op("dve", lambda e, cb=cb, w=w: e.tensor_tensor(out=merged[:, cb, 512 * w:512 * w + 512], in0=tA[:], in1=sg[1][:],
                                                                  op=ALU.add),
                     reads=["tA", "sg1"], writes=[f"mg{w}"])
        S.barrier()

    hid = hT[:].rearrange("p k t -> p (k t)").rearrange("p (f t) -> p f t", f=32)
    yaf = yaT[:].rearrange("p k t -> p (k t)").bitcast(F32)
    fT = yaf[:, 4096:8192].rearrange("p (c t) -> p c t", c=8)
    b1f = big1[:].rearrange("p a b -> p (a b)")
    w1b = [b1f[:, 8192 + 4096 * i:8192 + 4096 * (i + 1)].rearrange("p (k c) -> p k c", k=8) for i in range(2)]
    with ExitStack() as ph:
        xo = [sb(ph, f"xo{i}", [128, DM], F32) for i in range(2)]
        x1b = b1f[:, 0:8192].bitcast(F32).rearrange("p (t d) -> p t d", t=4)
        x1s = [yaf[:, 0:4096].rearrange("p (t d) -> p t d", t=4), x1b]
        t1 = sb(ph, "t1", [128, DM], F32)
        xn3 = [sb(ph, f"xn3{i}", [128, DM], BF16) for i in range(4)]
        junk3 = sb(ph, "junk3", [128, 512], BF16)
        h2T = sb(ph, "h2T", [128, 8, 512], BF16)
        w2b = [sb(ph, f"w2b{i}", [128, 32, 128], BF16) for i in range(2)]
        st3 = sb(ph, "st3", [128, 16], F32)
        xoctr = [0]

        def post_norm(pk0, gidx, addin, addkey, outap, outkey, scol):
            sk = f"st3_{scol}"
            for hf in range(2):
                S.op("act", lambda e, hf=hf: e.activation(out=junk3[:], in_=ps[pk0 + hf][:, :], func=AF.Square,
                                                          accum_out=st3[:, scol + hf:scol + hf + 1]),
                     reads=[], writes=[PK[pk0 + hf], "junk3", sk])
            S.op("dve", lambda e: e.tensor_tensor(out=st3[:, scol:scol + 1], in0=st3[:, scol:scol + 1], in1=st3[:, scol + 1:scol + 2],
                                                  op=ALU.add), reads=[], writes=[sk])
            S.op("act", lambda e: e.activation(out=st3[:, scol:scol + 1], in_=st3[:, scol:scol + 1], func=AF.Ln, scale=1.0 / DM, bias=EPS),
                 reads=[], writes=[sk])
            S.op("act", lambda e: e.activation(out=st3[:, scol:scol + 1], in_=st3[:, scol:scol + 1], func=AF.Exp, scale=-0.5),
                 reads=[], writes=[sk])
            for hf in range(2):
                S.op("dve", lambda e, hf=hf: e.scalar_tensor_tensor(
                    out=t1[:, 512 * hf:512 * hf + 512], in0=ps[pk0 + hf][:, :], scalar=st3[:, scol:scol + 1],
                    in1=gb3[:, gidx, 512 * hf:512 * hf + 512], op0=ALU.mult, op1=ALU.mult),
                    reads=[sk, "gb3"], writes=[PK[pk0 + hf], "t1"])
            S.op("dve", lambda e: e.tensor_tensor(out=outap, in0=t1[:], in1=addin, op=ALU.add),
                 reads=["t1", addkey], writes=[outkey])

        def front_mm(w, ti):
            i = 4 * w + ti
            x1 = x1s[w % 2]
            b = xoctr[0] % 2
            xoctr[0] += 1
            S.dma("sp", lambda e: e.dma_start(out=xo[b][:], in_=x_d[128 * i:128 * i + 128, :]), f"xo{b}", writes=[f"xo{b}"])
            pk0 = 2 * (ti % 2)
            for hf in range(2):
                for kc in range(8):
                    mm(ps[pk0 + hf][:, :], merged[:, kc, 128 * i:128 * i + 128], wout[:, kc, 512 * hf:512 * hf + 512],
                       kc == 0, kc == 7, [f"mg{w}", "wout"], [PK[pk0 + hf]], kc == 7)
            x1k = f"x1_{w % 2}_{ti}"
            post_norm(pk0, 0, xo[b][:], f"xo{b}", x1[:, ti, :], x1k, 4 * (ti % 2))
            sc = 8 + ti
            S.op("act", lambda e: e.activation(out=xn3[ti][:], in_=x1[:, ti, :], func=AF.Square, accum_out=st3[:, sc:sc + 1]),
                 reads=[x1k], writes=[f"xn3{ti}", f"st3_{sc}"])
            S.op("act", lambda e: e.activation(out=st3[:, sc:sc + 1], in_=st3[:, sc:sc + 1], func=AF.Ln, scale=1.0 / DM, bias=EPS),
                 reads=[], writes=[f"st3_{sc}"])
            S.op("act", lambda e: e.activation(out=st3[:, sc:sc + 1], in_=st3[:, sc:sc + 1], func=AF.Exp, scale=-0.5),
                 reads=[], writes=[f"st3_{sc}"])
            S.op("dve", lambda e: e.scalar_tensor_tensor(out=xn3[ti][:], in0=x1[:, ti, :], scalar=st3[:, sc:sc + 1], in1=gb3[:, 1, :],
                                                         op0=ALU.mult, op1=ALU.mult),
                 reads=[x1k, f"st3_{sc}", "gb3"], writes=[f"xn3{ti}"])

        def front_tr(w, ti):
            pk = 6 + ti % 2
            pbf = ps[pk][:].bitcast(BF16)
            for kc in range(8):
                S.op("pe", lambda e, kc=kc: e.transpose(out=pbf[:, kc * 128:(kc + 1) * 128],
                                                        in_=xn3[ti][:, kc * 128:(kc + 1) * 128], identity=ident[:]),
                     reads=[f"xn3{ti}", "ident"], writes=[PK[pk]], signal=(kc == 7))
            S.op("act", lambda e: e.activation(out=h2T[:, :, 128 * ti:128 * ti + 128],
                                               in_=pbf.rearrange("p (k t) -> p k t", k=8), func=AF.Copy),
                 reads=[], writes=[PK[pk], "h2T"])

        def tail_tile(w, ti):
            x1 = x1s[w % 2]
            i = 4 * w + ti
            pk0 = 2 * (ti % 2)
            for cb in range(8):
                S.op("pe", lambda e, cb=cb: e.transpose(
                    out=ps[pk0 + cb // 4][:, 128 * (cb % 4):128 * (cb % 4) + 128], in_=fT[:, cb, 128 * ti:128 * ti + 128],
                    identity=identf[:]), reads=FT + ["identf"], writes=[PK[pk0 + cb // 4]], signal=(cb % 4 == 3))
            ob = xoctr[0] % 2
            xoctr[0] += 1
            post_norm(pk0, 2, x1[:, ti, :], f"x1_{w % 2}_{ti}", xo[ob][:], f"xo{ob}", 4 * (ti % 2) + 2)
            S.dma("sp", lambda e: e.dma_start(out=out_d[128 * i:128 * i + 128, :], in_=xo[ob][:]), f"xo{ob}",
                  reads=[f"xo{ob}"], writes=[f"outd{ob}"])

        front_mm(0, 0)
        front_mm(0, 1)
        front_tr(0, 0)
        front_mm(0, 2)
        front_tr(0, 1)
        front_mm(0, 3)
        front_tr(0, 2)
        front_tr(0, 3)
        HID = [f"hid{fb}" for fb in range(32)]
        FT = [f"fT{cb}" for cb in range(8)]
        for w in range(4):
            for fq in range(8):
                b = fq % 2
                S.dma("pool", lambda e, fq=fq, b=b: e.dma_start(out=w1b[b], in_=w1s_d[fq].rearrange("p (k c) -> p k c", k=8)),
                      f"w1b{b}", writes=[f"w1b{b}"])
                for f4 in range(4):
                    fb = 4 * fq + f4
                    pk = 4 + fb % 4
                    for kc in range(8):
                        mm(ps[pk][:, :], w1b[b][:, kc, 128 * f4:128 * f4 + 128], h2T[:, kc, :], kc == 0, kc == 7,
                           [f"w1b{b}", "h2T"], [PK[pk]], kc == 7)
                    S.op("act", lambda e, fb=fb, pk=pk: e.activation(out=hid[:, fb, :], in_=ps[pk][:, :], func=AF.Relu),
                         reads=[], writes=[PK[pk], f"hid{fb}"])
                    S.op("dve", lambda e, fb=fb: e.tensor_tensor(out=hid[:, fb, :], in0=hid[:, fb, :], in1=hid[:, fb, :], op=ALU.mult),
                         reads=[], writes=[f"hid{fb}"])
                if w > 0 and fq % 2 == 0:
                    tail_tile(w - 1, fq // 2)
                if w < 3 and fq % 2 == 1:
                    front_mm(w + 1, fq // 2)
            for cb in range(8):
                b = cb % 2
                S.dma("pool", lambda e, cb=cb, b=b: e.dma_start(
                    out=w2b[b][:], in_=w2s_d[cb].rearrange("p (f c) -> p f c", f=32)), f"w2b{b}", writes=[f"w2b{b}"])
                pk = 4 + cb % 2
                for fb in range(32):
                    mm(ps[pk][:, :], w2b[b][:, fb, :], hid[:, fb, :], fb == 0, fb == 31, [f"w2b{b}"] + (HID if fb == 0 else []),
                       [PK[pk]], fb == 31)
                S.op("act", lambda e, cb=cb, pk=pk: e.activation(out=fT[:, cb, :], in_=ps[pk][:, :], func=AF.Copy),
                     reads=[], writes=[PK[pk], f"fT{cb}"])
                if w < 3 and cb % 2 == 0:
                    front_tr(w + 1, cb // 2)
            if w == 3:
                for ti in range(4):
                    tail_tile(3, ti)
        S.barrier()
    late.close()
    top.close()
    S.emit()
    return nc


_CACHE = {}


def _consts():
    if "c" in _CACHE:
        return _CACHE["c"]
    bf = ml_dtypes.bfloat16
    ident = np.eye(128, dtype=np.float32)
    k = np.arange(128)[:, None].astype(np.float64)
    q = np.arange(128)[None, :].astype(np.float64)
    mask = (k <= q).astype(np.float32)
    eb = np.zeros((128, 80, 128), np.float32)
    dils = (1.0, 4.0, 16.0)
    for h in range(16):
        slope = 2.0 ** (-8.0 * (h + 1) / 16.0)
        for p in range(3):
            cur = np.where(k <= q, np.exp(-slope * dils[p] * (q - k)), 0.0)
            prev = np.where(k >= q, np.exp(-slope * dils[p] * (q + 128.0 - k)), 0.0)
            if p < 2:
                eb[:, h * 5 + 2 * p, :] = cur
                eb[:, h * 5 + 2 * p + 1, :] = prev
            else:
                eb[:, h * 5 + 4, :] = cur
    c = {"ident": ident.astype(bf), "identf": ident, "mask": mask, "onesf": np.ones((128, 128), np.float32),
         "eb": eb.astype(bf)}
    _CACHE["c"] = c
    return c


def kernel(x, norm_mix_pre, w_in, b_gate, ln_v_g, ln_v_b, w_s, b_s, w_a_proj, w_b_proj, w_out,
           norm_mix_post, norm_ffn_pre, w_ff1, w_ff2, norm_ffn_post):
    f = lambda a: np.ascontiguousarray(np.asarray(a, dtype=np.float32))
    x = f(x)

    def kmaj(w, k):
        w = f(w)
        return np.ascontiguousarray(w.reshape(k, 128, w.shape[-1]).transpose(1, 0, 2))

    shared = dict(_consts())
    def kblk(w):
        w = f(w)
        nb = w.shape[1] // 128
        return np.ascontiguousarray(w.reshape(8, 128, nb, 128).transpose(2, 1, 0, 3)).reshape(nb, 128, 1024)

    shared["w_in"] = kblk(np.asarray(w_in)[0])
    shared["w_a"] = kblk(np.asarray(w_a_proj)[0])
    shared["w_b"] = kblk(np.asarray(w_b_proj)[0])
    shared["w_o"] = kmaj(np.asarray(w_out)[0], 8)
    shared["w_1"] = kmaj(np.asarray(w_ff1)[0], 8)
    w2 = f(np.asarray(w_ff2)[0]).reshape(2, 16, 128, 8, 128).transpose(3, 0, 2, 1, 4)
    shared["w_2"] = np.ascontiguousarray(w2).reshape(16, 128, 2048)
    shared["wsT"] = np.ascontiguousarray(f(w_s)[0].transpose(2, 0, 1))
    shared["b_s"] = f(b_s)[0].reshape(1, 1024)
    shared["ln_g_row"] = f(ln_v_g)[0].reshape(1, 1024)
    shared["ln_b"] = np.ascontiguousarray(f(ln_v_b)[0].reshape(8, 128).T)
    shared["b_gate"] = np.ascontiguousarray(f(b_gate)[0].reshape(2, 8, 128).transpose(2, 0, 1).reshape(128, 16))
    shared["gains"] = np.ascontiguousarray(np.stack([f(norm_mix_pre)[0], f(norm_mix_post)[0], f(norm_ffn_pre)[0],
                                                     f(norm_ffn_post)[0]], axis=0))
    if "nc" not in _CACHE:
        _CACHE["nc"] = build_nc()
    nc = _CACHE["nc"]
    in_maps = []
    for c in range(NCORES):
        m = dict(shared)
        m["x"] = x[c]
        in_maps.append(m)
    res = run_bass_kernel_spmd(nc, in_maps, core_ids=list(range(NCORES)))
    return np.stack([np.asarray(r["out"], dtype=np.float32) for r in res.results], axis=0)
```
